# Optimizing a Trainium2 kernel written in Bass

```python
import jax
import jax.numpy as jnp
from jax import lax
import numpy as np

D_MODEL = 1024
BATCH = 8
SEQ = 2048
DEPTH = 1

N_META = 16
GLA_WIDTH = D_MODEL // 2
GLA_HEADS = 4
GLA_DV = GLA_WIDTH // GLA_HEADS
GLA_DK = GLA_DV // 2
GLA_QK = GLA_HEADS * GLA_DK
GATE_RANK = 16
GATE_NORM = 16.0
CHUNK = 64
SUB = 16
CONV_CH = D_MODEL - GLA_WIDTH
CONV_K = 3
D_FF = 4 * D_MODEL
EPS = 1e-6
PROJ_SIZES = (GLA_QK, GLA_QK, GLA_WIDTH, GLA_WIDTH, GATE_RANK, CONV_CH, CONV_CH, CONV_CH)
PROJ_WIDTH = 2 * GLA_QK + 2 * GLA_WIDTH + GATE_RANK + 3 * CONV_CH

kernel_name = "hymba_gla_shortconv_block"


def rms_norm(x, w):
    xf = x.astype(jnp.float32)
    y = xf * lax.rsqrt(jnp.mean(xf * xf, axis=-1, keepdims=True) + EPS)
    return (y * w.astype(jnp.float32)).astype(x.dtype)


def split_cols(p):
    outs = []
    start = 0
    for s in PROJ_SIZES:
        outs.append(p[..., start:start + s])
        start += s
    return outs


def gla_chunked(q, k, v, gk):
    Bsz, Lp, H, dk = q.shape
    dv = v.shape[-1]
    N = Lp // CHUNK
    S = CHUNK // SUB

    def blk(t):
        return t.reshape(Bsz, N, S, SUB, H, t.shape[-1]).transpose(0, 4, 1, 2, 3, 5)

    q, k, v, gk = blk(q), blk(k), blk(v), blk(gk)
    b = jnp.cumsum(gk.reshape(Bsz, H, N, CHUNK, dk), axis=3).reshape(Bsz, H, N, S, SUB, dk)

    b_end = b[..., -1, :]
    r = jnp.concatenate([jnp.zeros_like(b_end[:, :, :, :1]), b_end[:, :, :, :-1]], axis=3)
    q_r = q * jnp.exp(b - r[..., None, :])
    lower = jnp.arange(S)[:, None] > jnp.arange(S)[None, :]
    expo = r[:, :, :, :, None, None, :] - b[:, :, :, None, :, :, :]
    expo = jnp.where(lower[:, :, None, None], expo, -jnp.inf)
    k_rel = k[:, :, :, None] * jnp.exp(expo)
    a_off = jnp.einsum('bhnsid,bhnstjd->bhnsitj', q_r, k_rel)

    causal = jnp.tril(jnp.ones((SUB, SUB), dtype=bool))
    pair = b[..., :, None, :] - b[..., None, :, :]
    pair = jnp.where(causal[:, :, None], pair, -jnp.inf)
    a_diag = jnp.einsum('bhnsid,bhnsjd,bhnsijd->bhnsij', q, k, jnp.exp(pair))
    eye = jnp.eye(S, dtype=a_off.dtype)
    a = a_off + a_diag[..., None, :] * eye[:, None, :, None]
    a = a.reshape(Bsz, H, N, CHUNK, CHUNK)

    qc = q.reshape(Bsz, H, N, CHUNK, dk)
    kc = k.reshape(Bsz, H, N, CHUNK, dk)
    vc = v.reshape(Bsz, H, N, CHUNK, dv)
    bc = b.reshape(Bsz, H, N, CHUNK, dk)
    o_intra = jnp.einsum('bhnij,bhnjv->bhniv', a, vc)

    b_last = bc[..., -1, :]
    kv = jnp.einsum('bhncd,bhncv->bhndv', kc * jnp.exp(b_last[..., None, :] - bc), vc)

    def step(h, inp):
        decay, kv_n = inp
        return h * decay[..., None] + kv_n, h

    h0 = jnp.zeros((Bsz, H, dk, dv), dtype=q.dtype)
    _, h_prev = lax.scan(step, h0, (jnp.moveaxis(jnp.exp(b_last), 2, 0), jnp.moveaxis(kv, 2, 0)))
    h_prev = jnp.moveaxis(h_prev, 0, 2)
    o_inter = jnp.einsum('bhncd,bhndv->bhncv', qc * jnp.exp(bc), h_prev)

    o = o_intra + o_inter
    return o.transpose(0, 2, 3, 1, 4).reshape(Bsz, Lp, H, dv)


def gla_branch(q, k, v, g, gr, w_gate_up, b_gate, norm_w):
    Bsz, L, _ = q.shape
    f32 = jnp.float32
    qh = q.astype(f32).reshape(Bsz, L, GLA_HEADS, GLA_DK) * (GLA_DK ** -0.5)
    kh = k.astype(f32).reshape(Bsz, L, GLA_HEADS, GLA_DK)
    vh = v.astype(f32).reshape(Bsz, L, GLA_HEADS, GLA_DV)
    gk = jax.nn.log_sigmoid(gr.astype(f32) @ w_gate_up.astype(f32) + b_gate.astype(f32)) / GATE_NORM
    gk = gk.reshape(Bsz, L, GLA_HEADS, GLA_DK)
    front = (-N_META) % CHUNK
    back = (-(front + L)) % CHUNK

    def padf(t):
        return jnp.pad(t, ((0, 0), (front, back), (0, 0), (0, 0)))

    o = gla_chunked(padf(qh), padf(kh), padf(vh), padf(gk))[:, front:front + L]
    o = rms_norm(o, norm_w)
    o = o.reshape(Bsz, L, GLA_WIDTH) * jax.nn.silu(g.astype(f32))
    return o.astype(q.dtype)


def short_conv_branch(cb, cc, cx, conv_w):
    u = cc * cx
    y = lax.conv_general_dilated(
        u, conv_w[:, None, :].astype(u.dtype), window_strides=(1,),
        padding=[(CONV_K - 1, 0)], dimension_numbers=('NWC', 'WIO', 'NWC'),
        feature_group_count=CONV_CH)
    return cb * y


def setup_inputs(seed: int = 0) -> dict:
    key = jax.random.key(seed)
    ks = jax.random.split(key, 14)
    n = jax.random.normal
    return {
        'x': n(ks[0], (BATCH, SEQ, D_MODEL), jnp.float32),
        'meta_tokens': n(ks[1], (N_META, D_MODEL), jnp.float32),
        'norm_mix_w': 1.0 + 0.01 * n(ks[2], (DEPTH, D_MODEL), jnp.float32),
        'w_in': n(ks[3], (DEPTH, D_MODEL, PROJ_WIDTH), jnp.float32) * D_MODEL ** -0.5,
        'w_gate_up': n(ks[4], (DEPTH, GATE_RANK, GLA_QK), jnp.float32) * GATE_RANK ** -0.5,
        'b_gate': 0.1 * n(ks[5], (DEPTH, GLA_QK), jnp.float32),
        'gla_norm_w': 1.0 + 0.01 * n(ks[6], (DEPTH, GLA_DV), jnp.float32),
        'conv_w': n(ks[7], (DEPTH, CONV_K, CONV_CH), jnp.float32) * CONV_K ** -0.5,
        'w_out': n(ks[8], (DEPTH, D_MODEL, D_MODEL), jnp.float32) * D_MODEL ** -0.5,
        'norm_mlp_w': 1.0 + 0.01 * n(ks[9], (DEPTH, D_MODEL), jnp.float32),
        'w_up': n(ks[10], (DEPTH, D_MODEL, D_FF), jnp.float32) * D_MODEL ** -0.5,
        'w_down': n(ks[11], (DEPTH, D_FF, D_MODEL), jnp.float32) * D_FF ** -0.5,
        'norm_final_w': 1.0 + 0.01 * n(ks[12], (D_MODEL,), jnp.float32),
    }


def reference(x, meta_tokens, norm_mix_w, w_in, w_gate_up, b_gate, gla_norm_w, conv_w,
              w_out, norm_mlp_w, w_up, w_down, norm_final_w):
    Bsz = x.shape[0]
    meta = jnp.broadcast_to(meta_tokens.astype(x.dtype)[None], (Bsz, N_META, D_MODEL))
    h = jnp.concatenate([meta, x], axis=1)
    for layer in range(DEPTH):
        hn = rms_norm(h, norm_mix_w[layer])
        proj = hn @ w_in[layer]
        q, k, v, g, gr, cb, cc, cx = split_cols(proj)
        y_gla = gla_branch(q, k, v, g, gr, w_gate_up[layer], b_gate[layer], gla_norm_w[layer])
        y_conv = short_conv_branch(cb, cc, cx, conv_w[layer]).astype(h.dtype)
        mixed = jnp.concatenate([y_gla.astype(h.dtype), y_conv], axis=-1) @ w_out[layer]
        h = h + mixed
        hn = rms_norm(h, norm_mlp_w[layer])
        h = h + jnp.square(jax.nn.relu(hn @ w_up[layer])) @ w_down[layer]
    out = rms_norm(h, norm_final_w)
    return out[:, N_META:]
```

```python
import contextlib

import numpy as np

import concourse.bass as bass
import concourse.mybir as mybir
from concourse.bass_utils import run_bass_kernel_spmd

F32 = mybir.dt.float32
BF16 = mybir.dt.bfloat16
AF = mybir.ActivationFunctionType
ALU = mybir.AluOpType
AX = mybir.AxisListType

D = 1024
SEQ = 2048
NCH = 16
PW = 3088
DFF = 4096
EPS = 1e-6
C_Q, C_K, C_V, C_G, C_GR, C_CB, C_CC, C_CX = 0, 256, 512, 1024, 1536, 1552, 2064, 2576

ENGS = ("pe", "act", "dve", "pool", "sp")
LAYOUT = {}
DEBUG_STOP = None


class _Stop(Exception):
    pass


def chk(name):
    if DEBUG_STOP == name:
        raise _Stop()


class Node:
    __slots__ = ("eng", "fn", "deps", "sig", "cnt", "dma", "sem", "semval", "idx", "psdeps", "nofuse", "pure_ps")

    def __init__(self, eng, fn, dma):
        self.eng = eng
        self.fn = fn
        self.deps = []
        self.sig = False
        self.cnt = 0
        self.dma = dma
        self.sem = None
        self.semval = 0
        self.idx = 0
        self.psdeps = set()
        self.pure_ps = set()
        self.nofuse = False


class Prog:
    N_DMA_SEMS = 36

    def __init__(self, nc):
        self.nc = nc
        self.q = {e: [] for e in ENGS}
        self.last_w = {}
        self.readers = {}
        self.dma_nodes = []
        self.dma_by_q = {}
        self.n = 0
        self.fuse = True

    def alias(self, new, olds):
        r = self.readers.setdefault(new, {})
        for o in olds:
            w = self.last_w.get(o)
            if w is not None:
                r[("w", o)] = w
            for kk, nd in self.readers.get(o, {}).items():
                r[(o, kk)] = nd

    def op(self, eng, fn, reads=(), writes=(), dma=False, extra=(), nofuse=False):
        nd = Node(eng, fn, dma)
        nd.idx = self.n
        nd.nofuse = nofuse
        self.n += 1
        deps = list(extra)
        psd = []
        for k in reads:
            w = self.last_w.get(k)
            if w is not None:
                deps.append(w)
        for k in writes:
            is_ps = isinstance(k, tuple) and k[0] == "ps"
            w = self.last_w.get(k)
            if w is not None:
                deps.append(w)
                if is_ps:
                    psd.append(w)
            rr = list(self.readers.get(k, {}).values())
            deps.extend(rr)
            if is_ps:
                psd.extend(rr)
        nondps = set(id(d) for d in deps) - set(id(d) for d in psd)
        rk = ("dma", nd.idx) if dma else eng
        for k in reads:
            self.readers.setdefault(k, {})[rk] = nd
        for k in writes:
            self.last_w[k] = nd
            self.readers[k] = {}
        if dma:
            lst = self.dma_by_q.setdefault(eng, [])
            base, nslots = (0, 12) if eng == "pool" else (12, self.N_DMA_SEMS - 12)
            i = len(lst)
            if i >= nslots:
                deps.append(lst[i - nslots])
            nd.semval = 16 * (i // nslots + 1)
            nd.sem = base + i % nslots
            lst.append(nd)
            self.dma_nodes.append(nd)
        seen = set()
        for d in deps:
            if d is nd or id(d) in seen:
                continue
            if (not d.dma) and (not dma) and d.eng == "pe" and eng == "pe":
                continue
            seen.add(id(d))
            nd.deps.append(d)
            d.sig = True
        nd.pure_ps = set(id(d) for d in psd) - nondps
        self.q[eng].append(nd)
        return nd

    def emit(self, final_wait=()):
        nc = self.nc
        for d in final_wait:
            d.sig = True
        for e in ENGS:
            c = 0
            for nd in self.q[e]:
                if (not nd.dma) and nd.sig:
                    c += 1
                    nd.cnt = c
        with contextlib.ExitStack() as st:
            csem = {e: st.enter_context(nc.semaphore("c_" + e)) for e in ENGS}
            dsem = [st.enter_context(nc.semaphore("d_%d" % i)) for i in range(self.N_DMA_SEMS)]
            block = st.enter_context(nc.Block())

            def run_queue(e, eng):
                waited = {}

                def wait_for(d):
                    if d.dma:
                        key, sem, val = ("d", d.sem), dsem[d.sem], d.semval
                    else:
                        key, sem, val = ("c", d.eng), csem[d.eng], d.cnt
                    if waited.get(key, 0) >= val:
                        return
                    waited[key] = val
                    eng.wait_ge(sem, val)

                def need(d):
                    if d.dma:
                        key, sem, val = ("d", d.sem), dsem[d.sem], d.semval
                    else:
                        key, sem, val = ("c", d.eng), csem[d.eng], d.cnt
                    if waited.get(key, 0) >= val:
                        return None
                    return key, sem, val

                for nd in self.q[e]:
                    fused = None
                    if nd.dma or nd.nofuse or not self.fuse:
                        for d in nd.deps:
                            wait_for(d)
                    elif e == "pe":
                        for d in nd.deps:
                            if id(d) not in nd.pure_ps:
                                wait_for(d)
                        pend = {}
                        for d in nd.deps:
                            if id(d) in nd.pure_ps:
                                r = need(d)
                                if r is not None and (r[0] not in pend or pend[r[0]][2] < r[2]):
                                    pend[r[0]] = r
                        pend = list(pend.values())
                        for (key, sem, val) in pend[:-1]:
                            waited[key] = val
                            eng.wait_ge(sem, val)
                        if pend:
                            fused = pend[-1]
                    else:
                        pend = {}
                        for d in nd.deps:
                            r = need(d)
                            if r is not None and (r[0] not in pend or pend[r[0]][2] < r[2]):
                                pend[r[0]] = r
                        pend = list(pend.values())
                        for (key, sem, val) in pend[:-1]:
                            waited[key] = val
                            eng.wait_ge(sem, val)
                        if pend:
                            fused = pend[-1]
                    ins = nd.fn(eng)
                    if fused is not None:
                        key, sem, val = fused
                        waited[key] = val
                        ins._wait_ge(sem, val)
                    if nd.dma:
                        ins.then_inc(dsem[nd.sem], 16)
                    elif nd.sig:
                        ins.then_inc(csem[e], 1)
                if e == "sp":
                    for d in final_wait:
                        wait_for(d)

            @block.tensor
            def _(eng):
                run_queue("pe", eng)

            @block.scalar
            def _(eng):
                run_queue("act", eng)

            @block.vector
            def _(eng):
                run_queue("dve", eng)

            @block.gpsimd
            def _(eng):
                run_queue("pool", eng)

            @block.sync
            def _(eng):
                run_queue("sp", eng)


def build_program():
    nc = bass.Bass("TRN2", target_bir_lowering=False)

    def din(name, shape):
        return nc.dram_tensor(name, list(shape), F32, kind="ExternalInput").ap()

    x_d = din("x", (SEQ, D))
    xm_d = din("xm", (128, D))
    w_in_d = din("w_in", (D, PW))
    w_out_d = din("w_out", (D, D))
    w_up_d = din("w_up", (D, DFF))
    w_down_d = din("w_down", (DFF, D))
    wgu_d = din("wgu", (16, 256))
    nmw_d = din("nmw", (128, 8))
    nlw_d = din("nlw", (128, 8))
    nfw_d = din("nfw", (128, D))
    bg_d = din("bg", (128, 2))
    gnw_d = din("gnw", (128, 1))
    cw_d = din("cw", (128, 12))
    ident_d = din("ident", (128, 128))
    mask_d = din("mask", (128, 128))
    out_d = nc.dram_tensor("out", [SEQ, D], F32, kind="ExternalOutput").ap()

    ARENA_BYTES = 207 * 1024
    arena = nc.alloc_sbuf_tensor("arena", [128, ARENA_BYTES // 4], F32)
    cur = [0]

    def carve(nbytes, dt, shape=None, at=None, name=None):
        nb = (nbytes + 31) // 32 * 32
        if at is None:
            off = cur[0]
            cur[0] += nb
        else:
            off = at
        assert off % 4 == 0 and off + nb <= ARENA_BYTES, (off, nb)
        v = arena[:, off // 4:(off + nb) // 4]
        if dt is BF16:
            v = v.bitcast(BF16)[:, 0:nbytes // 2]
        else:
            v = v[:, 0:nbytes // 4]
        LAYOUT.setdefault('_list', []).append((off, nbytes))
        return v, off

    def t3(v, a):
        return v.rearrange("p (a b) -> p a b", a=a)

    Hf, _ = carve(NCH * D * 4, F32)
    H = t3(Hf, NCH)
    WINf, off_win = carve(8 * PW * 2, BF16)
    WIN = t3(WINf, 8)
    WOUTf, _ = carve(8 * D * 2, BF16)
    WOUT = t3(WOUTf, 8)
    IDB, _ = carve(128 * 2, BF16)
    MASK, _ = carve(128 * 4, F32)
    SCANM, _ = carve(512 * 4, F32)
    NMW, _ = carve(32, F32)
    NLW, _ = carve(32, F32)
    GNW, _ = carve(32, F32)
    CW, _ = carve(64, F32)
    BG, _ = carve(32, F32)
    NEGB, _ = carve(32, F32)
    WGU, _ = carve(256 * 2, BF16)
    SS, _ = carve(64, F32)
    RSTD, _ = carve(64, F32)
    SS4, _ = carve(32, F32)
    RS4, _ = carve(32, F32)
    CARRY, _ = carve(32, F32)
    Sf, LAYOUT['S'] = carve(2 * 128 * 4, F32)
    S = t3(Sf, 2)
    SBPf, _ = carve(2 * 128 * 2, BF16)
    SBP = t3(SBPf, 2)
    TMPf, _ = carve(2 * 128 * 4, F32)
    TMP = t3(TMPf, 2)
    XSB = [carve(D * 2, BF16)[0] for _ in range(4)]
    work0 = cur[0]
    LAYOUT['work0'] = work0
    HNTf, LAYOUT['HNT'] = carve(8 * 512 * 2, BF16)
    HNT = t3(HNTf, 8)
    E1f, LAYOUT['E1'] = carve(2 * 512 * 4, F32)
    E1 = t3(E1f, 2)
    E2f, LAYOUT['E2'] = carve(2 * 512 * 4, F32)
    E2 = t3(E2f, 2)
    CSf, LAYOUT['CS'] = carve(2 * 512 * 4, F32)
    CS = t3(CSf, 2)
    QZf, LAYOUT['QZ'] = carve(4 * 512 * 2, BF16)
    QZ = t3(QZf, 4)
    KTf, LAYOUT['KT'] = carve(2 * 512 * 2, BF16)
    KT = t3(KTf, 2)
    KTOKf, LAYOUT['KTOK'] = carve(4 * 512 * 2, BF16)
    KTOK = KTOKf.rearrange("p (c h d) -> p c h d", c=4, h=4)
    Vf, LAYOUT['V'] = carve(4 * 512 * 2, BF16)
    V = t3(Vf, 4)
    SGf, LAYOUT['SG'] = carve(4 * 512 * 4, F32)
    SG = t3(SGf, 4)
    ATB = [carve(512 * 2, BF16)[0] for _ in range(2)]
    YB = [carve(512 * 2, BF16)[0] for _ in range(2)]
    YTf, off_yt = carve(8 * 512 * 2, BF16)
    YT = t3(YTf, 8)
    LAYOUT['YT'] = off_yt
    XM, _ = carve(D * 4, F32, at=off_yt)
    HNTMf, _ = carve(8 * 128 * 2, BF16, at=off_yt + D * 4)
    HNTM = t3(HNTMf, 8)
    XSM, _ = carve(D * 2, BF16, at=off_yt + D * 4 + 8 * 128 * 2)
    CC, LAYOUT['CC'] = carve(512 * 4, F32)
    U, LAYOUT['U'] = carve(514 * 4, F32)
    T1, LAYOUT['T1'] = carve(512 * 4, F32)
    GR, LAYOUT['GR'] = carve(512 * 2, BF16)
    endA = cur[0]
    WU, WD = [], []
    o = off_win
    for i in range(2):
        a, _ = carve(8 * 1024 * 2, BF16, at=o)
        WU.append(t3(a, 8))
        o += 8 * 1024 * 2
        a, _ = carve(8 * 1024 * 2, BF16, at=o)
        WD.append(t3(a, 8))
        o += 8 * 1024 * 2
    o = work0
    HN2Tf, _ = carve(8 * SEQ * 2, BF16, at=o)
    HN2T = t3(HN2Tf, 8)
    o += 8 * SEQ * 2
    ACTB = []
    for i in range(2):
        a, _ = carve(8 * 512 * 2, BF16, at=o)
        ACTB.append(t3(a, 8))
        o += 8 * 512 * 2
    RB = []
    for i in range(2):
        a, _ = carve(512 * 4, F32, at=o)
        RB.append(a)
        o += 512 * 4
    NFW, _ = carve(D * 4, F32, at=o)
    o += D * 4
    assert o <= endA, (o, endA)
    LAYOUT['endA'] = endA
    A_KEYS = ["HNT", "E1", "E2", "CS", "QKT", "V", "SG", "AT", "SQ", "ON", "Y",
              "YT", "XM", "CC", "U", "T1", "GR"]

    PS = [nc.alloc_psum_tensor("ps%d" % i, [128, 512], F32) for i in range(8)]
    st = {"ps": 0, "nrot": 5, "psk": 0, "pso": 0}

    def next_ps():
        b = st["ps"] % st["nrot"]
        st["ps"] += 1
        return PS[b], ("ps", b)

    def next_psk():
        return PS[7], ("ps", 7)

    def next_pso():
        b = 5 + st["pso"] % 2
        st["pso"] += 1
        return PS[b], ("ps", b)

    def next_tp():
        ps, key = next_ps()
        return ps[:, :].bitcast(BF16), key

    P = Prog(nc)
    op = P.op

    def dma(eng, out, in_, writes, reads=(), extra=()):
        return op(eng, lambda e: e.dma_start(out=out, in_=in_), reads=reads, writes=writes, dma=True, extra=extra)

    for (dst, src, key) in ((NMW[:, 0:8], nmw_d, "NMW"), (BG[:, 0:2], bg_d, "BG"), (CW[:, 0:12], cw_d, "CW"),
                            (GNW[:, 0:1], gnw_d, "GNW"), (MASK, mask_d, "MASK"), (NLW[:, 0:8], nlw_d, "NLW")):
        dma("sp", dst, src, [key])
    dma("sp", XM, xm_d, ["XM"])
    dma("pool", IDB, ident_d, ["IDB"])
    dma("pool", WGU[0:16, :], wgu_d, ["WGU"])
    def load_x(t, extra=()):
        dma("sp", H[:, 4 * t:4 * t + 4, :], x_d[t * 512:(t + 1) * 512, :].rearrange("(c p) d -> p c d", p=128),
            [("H", 4 * t + c, hf) for c in range(4) for hf in range(2)], extra=extra)

    load_x(0)
    w_in_v = w_in_d.rearrange("(k p) n -> p k n", p=128)
    WIN_PIECES = [(C_GR, C_CC), (C_CC, C_CX), (C_CX, PW), (C_V, C_G), (C_G, C_GR), (C_Q, C_V)]

    def win_key(col):
        for i, (a, b) in enumerate(WIN_PIECES):
            if a <= col < b:
                return ("WIN", i)
        raise ValueError(col)

    for i in (0, 3, 5, 4, 1, 2):
        a, b = WIN_PIECES[i]
        dma("pool", WIN[:, :, a:b], w_in_v[:, :, a:b], [("WIN", i)])
    dma("pool", WOUT, w_out_d.rearrange("(k p) n -> p k n", p=128), ["WOUT"])

    op("pool", lambda e: e.memset(SCANM, 1.0), writes=["SCANM"])
    op("pool", lambda e: e.memset(SCANM.rearrange("p (c t) -> p c t", t=128)[:, :, 0:1], 0.0), writes=["SCANM"])
    op("pool", lambda e: e.memset(Sf, 0.0), writes=["S"])
    op("pool", lambda e: e.memset(SBPf, 0.0), writes=["SB"])
    op("pool", lambda e: e.memset(QZf, 0.0), writes=[("QZ", h) for h in range(4)])
    op("pool", lambda e: e.memset(KTOKf, 0.0), writes=[("KTOK", i) for i in range(4)])
    op("pool", lambda e: e.memset(CARRY[:, 0:8], 0.0), writes=["CARRY"])
    op("dve", lambda e: e.tensor_scalar(NEGB[:, 0:2], BG[:, 0:2], -1.0, None, ALU.mult), reads=["BG"], writes=["NEGB"])

    def rms_rstd(ss_ap, rs_ap, nfeat, keys_r, keys_w):
        op("act", lambda e: e.activation(rs_ap, ss_ap, AF.Ln, bias=EPS, scale=1.0 / nfeat), reads=keys_r, writes=keys_w)
        op("act", lambda e: e.activation(rs_ap, rs_ap, AF.Exp, scale=-0.5), reads=keys_w, writes=keys_w)

    SQJ, _ = carve(D * 2, BF16)

    def norm_chain(src, skeys, slot, part=0, scale_eng="act", xs=None, xkey=None):
        if part in (0, 1):
            op("act", lambda e: e.activation(SQJ, src, AF.Square, accum_out=SS[:, slot:slot + 1]), reads=skeys, writes=[("SS", slot)], nofuse=True)
            rms_rstd(SS[:, slot:slot + 1], RSTD[:, slot:slot + 1], D, [("SS", slot)], [("RSTD", slot)])
        xs = XSB[slot] if xs is None else xs
        xkey = ("XS", slot) if xkey is None else xkey
        if part in (0, 2):
            if scale_eng == "act":
                op("act", lambda e: e.activation(xs, src, AF.Copy, scale=RSTD[:, slot:slot + 1]),
                   reads=skeys + [("RSTD", slot)], writes=[xkey])
            else:
                op("dve", lambda e: e.tensor_scalar(xs, src, RSTD[:, slot:slot + 1], None, ALU.mult),
                   reads=skeys + [("RSTD", slot)], writes=[xkey])

    def transpose_chunk(slot, dst, dkey, wvec, wkey, xs=None, xkey=None):
        tp, tkey = next_tp()
        xb = XSB[slot] if xs is None else xs
        xkey = ("XS", slot) if xkey is None else xkey
        for k in range(8):
            op("pe", lambda e, k=k: e.transpose(tp[:, k * 128:(k + 1) * 128], xb[:, k * 128:(k + 1) * 128], IDB),
               reads=[xkey, "IDB"], writes=[tkey])
        wb = wvec[:, 0:8].unsqueeze(2).broadcast_to([128, 8, 128])
        op("dve", lambda e: e.tensor_tensor(dst, tp[:, :].rearrange("p (k t) -> p k t", k=8), wb, ALU.mult),
           reads=[tkey, wkey], writes=[dkey])

    def h_keys(ch):
        return [("H", ch, 0), ("H", ch, 1)]

    def trans_A(ncht):
        for c in range(ncht):
            transpose_chunk(c, HNT[:, :, c * 128:(c + 1) * 128], ("HNT", c), NMW, "NMW")

    pending = []

    def tile_A(t, meta, next_norm, next_trans):
        ncht = 1 if meta else 4
        TT = ncht * 128
        hnt = HNTM if meta else HNT
        hkey = "HNTM" if meta else "HNT"
        hnt_all = [(hkey, c) for c in range(ncht)]
        if (not meta) and t > 0:
            for c in range(4):
                transpose_chunk(c, HNT[:, :, c * 128:(c + 1) * 128], ("HNT", c), NMW, "NMW")
                yield "slot"

        def inproj_a(col0, m):
            ps, pkey = next_ps()
            wk = win_key(col0)
            for k in range(8):
                op("pe", lambda e, k=k: e.matmul(ps[0:m, 0:TT], WIN[:, k, col0:col0 + m], hnt[:, k, 0:TT],
                                                 start=(k == 0), stop=(k == 7)),
                   reads=hnt_all + [wk], writes=[pkey])
            return ps, pkey

        def inproj_b(col0, c):
            ps, pkey = next_ps()
            wk = win_key(col0)
            for k in range(8):
                op("pe", lambda e, k=k: e.matmul(ps[:, 0:512], hnt[:, k, c * 128:(c + 1) * 128], WIN[:, k, col0:col0 + 512],
                                                 start=(k == 0), stop=(k == 7)),
                   reads=[(hkey, c), wk], writes=[pkey])
            return ps, pkey

        ps, pkey = inproj_a(C_GR, 16)
        op("act", lambda e, ps=ps: e.copy(GR[0:16, 0:TT], ps[0:16, 0:TT]), reads=[pkey], writes=["GR"])

        def gate(c2):
            ps, pkey = next_ps()
            op("pe", lambda e: e.matmul(ps[:, 0:TT], WGU[0:16, c2 * 128:(c2 + 1) * 128], GR[0:16, 0:TT], start=True, stop=True),
               reads=["GR", "WGU"], writes=[pkey])
            op("act", lambda e: e.activation(E1[:, c2, 0:TT], ps[:, 0:TT], AF.Exp, scale=-1.0, bias=NEGB[:, c2:c2 + 1]),
               reads=[pkey, "NEGB"], writes=[("E1", c2)])
            op("act", lambda e: e.activation(E1[:, c2, 0:TT], E1[:, c2, 0:TT], AF.Ln, bias=1.0), reads=[("E1", c2)], writes=[("E1", c2)])
            op("dve", lambda e: e.tensor_tensor_scan(CS[:, c2, 0:TT], SCANM[:, 0:TT], E1[:, c2, 0:TT], 0.0, ALU.mult, ALU.add),
               reads=[("E1", c2), "SCANM"], writes=[("CS", c2)])
            op("act", lambda e: e.activation(E1[:, c2, 0:TT], CS[:, c2, 0:TT], AF.Exp, scale=-1.0 / 16.0),
               reads=[("CS", c2)], writes=[("E1", c2)])
            return op("act", lambda e: e.activation(E2[:, c2, 0:TT], CS[:, c2, 0:TT], AF.Exp, scale=1.0 / 16.0),
                      reads=[("CS", c2)], writes=[("E2", c2)])

        def v_chunk(c):
            ps, pkey = inproj_b(C_V, c)
            op("act", lambda e: e.copy(V[:, c, :], ps[:, 0:512]), reads=[pkey], writes=[("V", c)])

        def g_chunk(c):
            ps, pkey = inproj_b(C_G, c)
            sg = SG[:, c, :]
            k = ("SG", c)
            op("act", lambda e: e.activation(sg, ps[:, 0:512], AF.Exp, scale=-1.0), reads=[pkey], writes=[k])
            op("act", lambda e: e.activation(sg, sg, AF.Ln, bias=1.0), reads=[k], writes=[k])
            op("act", lambda e: e.activation(sg, sg, AF.Exp, scale=-1.0), reads=[k], writes=[k])
            op("dve", lambda e: e.tensor_tensor(sg, ps[:, 0:512], sg, ALU.mult), reads=[pkey, k], writes=[k])

        if not meta:
            yield "slot"
            v_chunk(0)
            yield "slot"
        gate(0)
        g1 = gate(1)
        if meta:
            load_x(1, extra=[g1])
        elif t + 2 < 4:
            load_x(t + 2, extra=[g1])
        if not meta:
            yield "slot"
        for c in range(0 if meta else 1, ncht):
            v_chunk(c)
            if not meta:
                yield "slot"
        if not meta:
            for c in range(ncht):
                g_chunk(c)

        def qk(i):
            ps, pkey = inproj_a((C_Q if i < 2 else C_K) + (i % 2) * 128, 128)
            if i < 2:
                for hh in range(2):
                    r = hh * 64
                    op("dve", lambda e, hh=hh, r=r: e.scalar_tensor_tensor(QZ[r:r + 64, 2 * i + hh, 0:TT], ps[r:r + 64, 0:TT], 0.125,
                                                                            E1[r:r + 64, i, 0:TT], ALU.mult, ALU.mult),
                       reads=[pkey, ("E1", i)], writes=[("QZ", 2 * i + hh)])
            else:
                p = i - 2
                op("dve", lambda e: e.tensor_tensor(KT[:, p, 0:TT], ps[:, 0:TT], E2[:, p, 0:TT], ALU.mult),
                   reads=[pkey, ("E2", p)], writes=[("KT", p)])

        for i in (2, 3, 0, 1):
            if meta and i < 2:
                continue
            qk(i)

        def ktrans(c):
            tp, tkey = next_tp()
            for p in range(2):
                op("pe", lambda e, p=p: e.transpose(tp[:, p * 128:(p + 1) * 128], KT[:, p, c * 128:(c + 1) * 128], IDB),
                   reads=[("KT", p), "IDB"], writes=[tkey])
            dst = KTOKf[:, c * 512:(c + 1) * 512].rearrange("q (p b d) -> q p b d", p=2, b=4)[:, :, 0:4:3, :]
            op("act", lambda e: e.copy(dst, tp[:, 0:256].rearrange("q (p h d) -> q p h d", p=2, h=2)), reads=[tkey], writes=[("KTOK", c)])

        for c in range(ncht):
            ktrans(c)
        chk("pre%d%s" % (t, "m" if meta else ""))

        st_ = {}

        def front(c):
            cs = slice(c * 128, (c + 1) * 128)
            if not meta:
                psA, kA = next_ps()
                for h in range(4):
                    op("pe", lambda e, h=h: e.matmul(psA[:, h * 128:(h + 1) * 128], KT[:, h // 2, cs], QZ[:, h, cs], start=True, stop=True),
                       reads=[("KT", h // 2), ("QZ", h)], writes=[kA])
                at = ATB[c % 2]
                op("dve", lambda e: e.tensor_tensor(at.rearrange("p (h i) -> p h i", h=4), psA[:, :].rearrange("p (h i) -> p h i", h=4),
                                                    MASK.unsqueeze(1).broadcast_to([128, 4, 128]), ALU.mult),
                   reads=[kA, "MASK"], writes=[("AT", c % 2)])

        def mid(c):
            cs = slice(c * 128, (c + 1) * 128)
            psK, kK = next_psk()
            for h in range(4):
                p = h // 2
                op("pe", lambda e, h=h, p=p: e.matmul(psK[:, p * 128:(p + 1) * 128], KTOK[:, c, h, :], V[:, c, h * 128:(h + 1) * 128],
                                                      start=(h % 2 == 0), stop=(h % 2 == 1)),
                   reads=[("KTOK", c), ("V", c)], writes=[kK])
            if not meta:
                psO, kO = next_pso()
                at = ATB[c % 2]
                for h in range(4):
                    op("pe", lambda e, h=h: e.matmul(psO[:, h * 128:(h + 1) * 128], at[:, h * 128:(h + 1) * 128],
                                                     V[:, c, h * 128:(h + 1) * 128], start=True, stop=False),
                       reads=[("AT", c % 2), ("V", c)], writes=[kO])
                    op("pe", lambda e, h=h: e.matmul(psO[:, h * 128:(h + 1) * 128], QZ[:, h, cs], SBP[:, h // 2, :], start=False, stop=True),
                       reads=[("QZ", h), "SB"], writes=[kO])
                st_[("O", c)] = (psO, kO)
            last = c * 128 + 127
            ebl_bc = E1[:, :, last:last + 1].broadcast_to([128, 2, 128])
            op("dve", lambda e: e.tensor_tensor(TMPf, Sf, psK[:, 0:256], ALU.add), reads=["S", kK], writes=["TMP"])
            op("dve", lambda e: e.tensor_tensor(SBP, TMP, ebl_bc, ALU.mult), reads=["TMP", ("E1", 0), ("E1", 1)], writes=["SB"])
            for p in range(2):
                op("act", lambda e, p=p: e.activation(S[:, p, :], TMP[:, p, :], AF.Copy, scale=E1[:, p, last:last + 1]),
                   reads=["TMP", ("E1", p)], writes=["S"])

        def back1(c):
            psO, kO = st_[("O", c)]
            SQ_, G2 = CS[:, 0, :], CS[:, 1, :]
            for h in range(4):
                op("act", lambda e, h=h: e.activation(SQ_[:, h * 128:(h + 1) * 128], psO[:, h * 128:(h + 1) * 128], AF.Square,
                                                      accum_out=SS4[:, h:h + 1]),
                   reads=[kO], writes=[("SS4", h)], nofuse=True)
            rms_rstd(SS4[:, 0:4], RS4[:, 0:4], 128, [("SS4", h) for h in range(4)], ["RS4"])

        def back1b(c):
            psO, kO = st_[("O", c)]
            G2 = CS[:, 1, :]
            op("pool", lambda e: e.tensor_tensor(G2.rearrange("p (h v) -> p h v", h=4), SG[:, c, :].rearrange("p (h v) -> p h v", h=4),
                                                 RS4[:, 0:4].unsqueeze(2).broadcast_to([128, 4, 128]), ALU.mult),
               reads=[("SG", c), "RS4"], writes=[("CS", 1)])
            op("dve", lambda e: e.tensor_tensor(YB[c % 2], psO[:, :], G2, ALU.mult), reads=[kO, ("CS", 1)], writes=[("Y", c % 2)])

        def back2(c):
            cs = slice(c * 128, (c + 1) * 128)
            tpy, ktpy = next_tp()
            y = YB[c % 2]
            for h in range(4):
                op("pe", lambda e, h=h: e.transpose(tpy[:, h * 128:(h + 1) * 128], y[:, h * 128:(h + 1) * 128], IDB),
                   reads=[("Y", c % 2), "IDB"], writes=[ktpy])
            wr = [("YT", h, c) for h in range(4)] + (["XM", ("HNTM", 0), "XSM"] if t == 0 else [])
            op("act", lambda e: e.activation(YT[:, 0:4, cs], tpy[:, 0:512].rearrange("p (h i) -> p h i", h=4), AF.Copy, scale=GNW[:, 0:1]),
               reads=[ktpy, "GNW"], writes=wr)

        def convj(j):
            ps_cc, k_cc = inproj_a(C_CC + j * 128, 128)
            op("act", lambda e: e.copy(CC[:, 0:TT], ps_cc[:, 0:TT]), reads=[k_cc], writes=["CC"])
            ps_cx, k_cx = inproj_a(C_CX + j * 128, 128)
            op("dve", lambda e: e.tensor_copy(U[:, 0:2], CARRY[:, 2 * j:2 * j + 2]), reads=["CARRY"], writes=["U"])
            op("dve", lambda e: e.tensor_tensor(U[:, 2:2 + TT], ps_cx[:, 0:TT], CC[:, 0:TT], ALU.mult), reads=[k_cx, "CC"], writes=["U"])
            op("dve", lambda e: e.tensor_copy(CARRY[:, 2 * j:2 * j + 2], U[:, TT:TT + 2]), reads=["U"], writes=["CARRY"])
            if meta:
                return
            ps_cb, k_cb = inproj_a(C_CB + j * 128, 128)
            op("act", lambda e: e.activation(T1[:, 0:TT], U[:, 2:2 + TT], AF.Copy, scale=CW[:, 3 * j + 2:3 * j + 3]),
               reads=["U", "CW"], writes=["T1"])
            op("dve", lambda e: e.scalar_tensor_tensor(T1[:, 0:TT], U[:, 1:1 + TT], CW[:, 3 * j + 1:3 * j + 2], T1[:, 0:TT], ALU.mult, ALU.add),
               reads=["U", "CW", "T1"], writes=["T1"])
            op("dve", lambda e: e.scalar_tensor_tensor(T1[:, 0:TT], U[:, 0:TT], CW[:, 3 * j:3 * j + 1], T1[:, 0:TT], ALU.mult, ALU.add),
               reads=["U", "CW", "T1"], writes=["T1"])
            wr = [("YT", 4 + j)] + (["XM", ("HNTM", 0), "XSM"] if t == 0 else [])
            op("dve", lambda e: e.tensor_tensor(YT[:, 4 + j, 0:TT], ps_cb[:, 0:TT], T1[:, 0:TT], ALU.mult), reads=[k_cb, "T1"], writes=wr)

        def outproj(c, hf):
            cs = slice(c * 128, (c + 1) * 128)
            psM, kM = next_ps()
            for f in range(8):
                rd = [("YT", f, c)] if f < 4 else [("YT", f)]
                op("pe", lambda e, f=f: e.matmul(psM[:, 0:512], YT[:, f, cs], WOUT[:, f, hf * 512:(hf + 1) * 512], start=(f == 0), stop=(f == 7)),
                   reads=rd + ["WOUT"], writes=[kM])
            hk = ("H", 4 * t + c, hf)
            hv = H[:, 4 * t + c, hf * 512:(hf + 1) * 512]
            op("dve", lambda e: e.tensor_tensor(hv, hv, psM[:, 0:512], ALU.add), reads=[kM, hk], writes=[hk])

        if meta:
            front(0)
            mid(0)
            yield "split"
            for j in range(4):
                convj(j)
            return
        yield "flush"
        front(0)
        mid(0)
        front(1)
        for c in range(3):
            back1(c)
            convj(c)
            back1b(c)
            if c == 0:
                next_norm(0)
            next_norm(c + 1)
            mid(c + 1)
            if c >= 1:
                back2(c - 1)
            if c + 2 < 4:
                front(c + 2)
            chk("chunk%d_%d" % (t, c))
        back1(3)
        convj(3)
        back1b(3)
        back2(2)
        pending.extend([lambda: outproj(0, 0), lambda: outproj(0, 1), lambda: outproj(1, 0), lambda: outproj(1, 1),
                        lambda: back2(3), lambda: outproj(2, 0), lambda: outproj(2, 1), lambda: outproj(3, 0),
                        lambda: outproj(3, 1)])

    try:
        norm_chain(XM, ["XM"], 4, xs=XSM, xkey="XSM")
        for c in range(4):
            norm_chain(H[:, c, :], h_keys(c), c)
        transpose_chunk(0, HNTM, ("HNTM", 0), NMW, "NMW", xs=XSM, xkey="XSM")
        trans_A(4)
        def drain(n=None):
            while pending and (n is None or n > 0):
                pending.pop(0)()
                if n is not None:
                    n -= 1

        def run(gen, until=None):
            for tag in gen:
                if tag == "slot":
                    drain(1)
                elif tag == "flush":
                    drain()
                if tag == until:
                    return

        gens = []
        for t in range(4):
            if t < 3:
                nn = lambda c, t=t: norm_chain(H[:, 4 * (t + 1) + c, :], h_keys(4 * (t + 1) + c), c)
            else:
                nn = lambda c: norm_chain(H[:, c, :], h_keys(c), c)
            gens.append(tile_A(t, False, nn, None))
        gm = tile_A(0, True, None, None)
        run(gm, until="split")
        run(gens[0], until="flush")
        run(gm)
        run(gens[0])
        for t in range(1, 4):
            run(gens[t])

        st["nrot"] = 8
        win_keys = [("WIN", i) for i in range(len(WIN_PIECES))]
        P.alias(("WU", 0), win_keys)
        P.alias(("WD", 0), win_keys)
        allA = (["XM", ("HNTM", 0), "XSM", "CC", "U", "T1", "GR"] + [("YT", 4 + j) for j in range(4)] +
                [("YT", h, c) for h in range(4) for c in range(4)] + [("HNT", c) for c in range(4)] +
                [(n, c2) for n in ("E1", "E2", "CS") for c2 in range(2)] + [("KT", i) for i in range(2)] +
                [("QZ", i) for i in range(4)] + [("KTOK", i) for i in range(4)] + [("V", c) for c in range(4)] +
                [("SG", c) for c in range(4)] + [("AT", i) for i in range(2)] + [("Y", i) for i in range(2)])
        for t in range(4):
            for c in range(4):
                P.alias(("HN2T", t, c), allA)

        w_up_v = w_up_d.rearrange("(k p) n -> p k n", p=128)

        def load_mlp_weights(p):
            b = p % 2
            dma("pool", WU[b], w_up_v[:, :, p * 1024:(p + 1) * 1024], [("WU", b)])
            dma("pool", WD[b], w_down_d[p * 1024:(p + 1) * 1024, :].rearrange("(j q) n -> q j n", q=128), [("WD", b)])

        load_mlp_weights(0)
        drain(2)
        for c in range(4):
            transpose_chunk(c, HN2T[:, :, c * 128:(c + 1) * 128], ("HN2T", 0, c), NLW, "NLW")
            drain(2)
        drain()
        P.alias(("WU", 1), win_keys + ["WOUT"])
        P.alias(("WD", 1), win_keys + ["WOUT"])
        for i in range(2):
            for j in range(8):
                P.alias(("ACTB", i, j), allA)
            P.alias(("RB", i), allA)
        P.alias("NFW", allA)
        dma("sp", NFW, nfw_d, ["NFW"])

        rb_i = [0]

        def stage1(b, t, j, slot):
            ps, pkey = next_ps()
            for k in range(8):
                op("pe", lambda e, k=k: e.matmul(ps[:, 0:512], WU[b][:, k, j * 128:(j + 1) * 128],
                                                 HN2T[:, k, t * 512:(t + 1) * 512], start=(k == 0), stop=(k == 7)),
                   reads=[("WU", b)] + [("HN2T", t, c_) for c_ in range(4)], writes=[pkey])
            ri = rb_i[0] % 2
            rb_i[0] += 1
            op("act", lambda e: e.activation(RB[ri], ps[:, 0:512], AF.Relu), reads=[pkey], writes=[("RB", ri)])
            op("pool", lambda e: e.tensor_tensor(ACTB[slot][:, j, :], RB[ri], RB[ri], ALU.mult),
               reads=[("RB", ri)], writes=[("ACTB", slot, j)])

        def stage2(b, t, c, hf, slot):
            ps, pkey = next_ps()
            for j in range(8):
                op("pe", lambda e, j=j: e.matmul(ps[:, 0:512], ACTB[slot][:, j, c * 128:(c + 1) * 128],
                                                 WD[b][:, j, hf * 512:(hf + 1) * 512], start=(j == 0), stop=(j == 7)),
                   reads=[("ACTB", slot, j), ("WD", b)], writes=[pkey])
            hk = ("H", 4 * t + c, hf)
            hv = H[:, 4 * t + c, hf * 512:(hf + 1) * 512]
            op("dve", lambda e: e.tensor_tensor(hv, hv, ps[:, 0:512], ALU.add), reads=[pkey, hk], writes=[hk])

        def final_norm(ch):
            hks = h_keys(ch)
            op("act", lambda e: e.activation(SQJ, H[:, ch, :], AF.Square, accum_out=SS[:, ch:ch + 1]), reads=hks, writes=[("SS", ch)], nofuse=True)
            rms_rstd(SS[:, ch:ch + 1], RSTD[:, ch:ch + 1], D, [("SS", ch)], [("RSTD", ch)])
            op("dve", lambda e: e.scalar_tensor_tensor(H[:, ch, :], H[:, ch, :], RSTD[:, ch:ch + 1], NFW, ALU.mult, ALU.mult),
               reads=hks + [("RSTD", ch), "NFW"], writes=hks)

        def store_chunk(ch):
            dma("sp", out_d[ch * 128:(ch + 1) * 128, :], H[:, ch, :], writes=[], reads=h_keys(ch))

        def trans_B(t):
            for c in range(4):
                transpose_chunk(c, HN2T[:, :, t * 512 + c * 128:t * 512 + (c + 1) * 128], ("HN2T", t, c), NLW, "NLW")

        def do_s1(k):
            p, t = divmod(k, 4)
            b, slot = p % 2, k % 2
            for j in range(8):
                stage1(b, t, j, slot)
                if p == 0 and t < 3:
                    ch = 4 * (t + 1) + j // 2
                    norm_chain(H[:, ch, :], h_keys(ch), j // 2, part=1 + j % 2, scale_eng="dve")
            if p == 0 and t < 3:
                trans_B(t + 1)

        def do_s2(k):
            p, t = divmod(k, 4)
            b, slot = p % 2, k % 2
            for c in range(4):
                for hf in range(2):
                    stage2(b, t, c, hf, slot)
                if p == 3:
                    final_norm(4 * t + c)
                    store_chunk(4 * t + c)
            if t == 3 and p + 2 < 4:
                load_mlp_weights(p + 2)

        load_mlp_weights(1)
        for k in range(3):
            do_s1(k)
            do_s2(k)
        do_s1(3)
        for k in range(3, 16):
            if k + 1 < 16:
                do_s1(k + 1)
            do_s2(k)
    except _Stop:
        P.emit(final_wait=list(P.dma_nodes) + [q[-1] for q in P.q.values() if q])
        return nc

    outs = [n for n in P.dma_nodes if n.eng == "sp"][-16:]
    P.emit(final_wait=outs)
    return nc


_NC = None


def kernel(x, meta_tokens, norm_mix_w, w_in, w_gate_up, b_gate, gla_norm_w, conv_w, w_out,
           norm_mlp_w, w_up, w_down, norm_final_w):
    global _NC
    f = lambda a: np.ascontiguousarray(np.asarray(a, dtype=np.float32))
    x = f(x)
    nb = x.shape[0]
    xm = np.zeros((128, D), np.float32)
    xm[112:] = f(meta_tokens)
    shared = {
        "xm": xm,
        "w_in": f(np.asarray(w_in)[0]),
        "w_out": f(np.asarray(w_out)[0]),
        "w_up": f(np.asarray(w_up)[0]),
        "w_down": f(np.asarray(w_down)[0]),
        "wgu": f(np.asarray(w_gate_up)[0]),
        "nmw": f(np.asarray(norm_mix_w)[0].reshape(8, 128).T),
        "nlw": f(np.asarray(norm_mlp_w)[0].reshape(8, 128).T),
        "nfw": f(np.broadcast_to(np.asarray(norm_final_w)[None, :], (128, D))),
        "bg": f(np.asarray(b_gate)[0].reshape(2, 128).T),
        "gnw": f(np.asarray(gla_norm_w)[0].reshape(128, 1)),
        "cw": f(np.asarray(conv_w)[0].reshape(3, 4, 128).transpose(2, 1, 0).reshape(128, 12)),
        "ident": np.eye(128, dtype=np.float32),
        "mask": np.triu(np.ones((128, 128), np.float32)),
    }
    if _NC is None:
        _NC = build_program()
    in_maps = [dict(shared, x=x[b]) for b in range(nb)]
    res = run_bass_kernel_spmd(_NC, in_maps, core_ids=list(range(nb)))
    return np.stack([np.asarray(r["out"], dtype=np.float32) for r in res.results], axis=0)
```

```python
import contextlib

import numpy as np

import concourse.bass as bass
import concourse.mybir as mybir
from concourse.bass_utils import run_bass_kernel_spmd

F32 = mybir.dt.float32
BF16 = mybir.dt.bfloat16
AF = mybir.ActivationFunctionType
ALU = mybir.AluOpType
AX = mybir.AxisListType

D = 1024
SEQ = 2048
NCH = 16
PW = 3088
DFF = 4096
EPS = 1e-6
C_Q, C_K, C_V, C_G, C_GR, C_CB, C_CC, C_CX = 0, 256, 512, 1024, 1536, 1552, 2064, 2576

ENGS = ("pe", "act", "dve", "pool", "sp")
LAYOUT = {}
DEBUG_STOP = None


class _Stop(Exception):
    pass


def chk(name):
    if DEBUG_STOP == name:
        raise _Stop()


class Node:
    __slots__ = ("eng", "fn", "deps", "sig", "cnt", "dma", "sem", "semval", "idx", "psdeps", "nofuse", "pure_ps")

    def __init__(self, eng, fn, dma):
        self.eng = eng
        self.fn = fn
        self.deps = []
        self.sig = False
        self.cnt = 0
        self.dma = dma
        self.sem = None
        self.semval = 0
        self.idx = 0
        self.psdeps = set()
        self.pure_ps = set()
        self.nofuse = False


class Prog:
    N_DMA_SEMS = 36

    def __init__(self, nc):
        self.nc = nc
        self.q = {e: [] for e in ENGS}
        self.last_w = {}
        self.readers = {}
        self.dma_nodes = []
        self.dma_by_q = {}
        self.n = 0
        self.fuse = True

    def alias(self, new, olds):
        r = self.readers.setdefault(new, {})
        for o in olds:
            w = self.last_w.get(o)
            if w is not None:
                r[("w", o)] = w
            for kk, nd in self.readers.get(o, {}).items():
                r[(o, kk)] = nd

    def op(self, eng, fn, reads=(), writes=(), dma=False, extra=(), nofuse=False):
        nd = Node(eng, fn, dma)
        nd.idx = self.n
        nd.nofuse = nofuse
        self.n += 1
        deps = list(extra)
        psd = []
        for k in reads:
            w = self.last_w.get(k)
            if w is not None:
                deps.append(w)
        for k in writes:
            is_ps = isinstance(k, tuple) and k[0] == "ps"
            w = self.last_w.get(k)
            if w is not None:
                deps.append(w)
                if is_ps:
                    psd.append(w)
            rr = list(self.readers.get(k, {}).values())
            deps.extend(rr)
            if is_ps:
                psd.extend(rr)
        nondps = set(id(d) for d in deps) - set(id(d) for d in psd)
        rk = ("dma", nd.idx) if dma else eng
        for k in reads:
            self.readers.setdefault(k, {})[rk] = nd
        for k in writes:
            self.last_w[k] = nd
            self.readers[k] = {}
        if dma:
            lst = self.dma_by_q.setdefault(eng, [])
            base, nslots = (0, 12) if eng == "pool" else (12, self.N_DMA_SEMS - 12)
            i = len(lst)
            if i >= nslots:
                deps.append(lst[i - nslots])
            nd.semval = 16 * (i // nslots + 1)
            nd.sem = base + i % nslots
            lst.append(nd)
            self.dma_nodes.append(nd)
        seen = set()
        for d in deps:
            if d is nd or id(d) in seen:
                continue
            if (not d.dma) and (not dma) and d.eng == "pe" and eng == "pe":
                continue
            seen.add(id(d))
            nd.deps.append(d)
            d.sig = True
        nd.pure_ps = set(id(d) for d in psd) - nondps
        self.q[eng].append(nd)
        return nd

    def emit(self, final_wait=()):
        nc = self.nc
        for d in final_wait:
            d.sig = True
        for e in ENGS:
            c = 0
            for nd in self.q[e]:
                if (not nd.dma) and nd.sig:
                    c += 1
                    nd.cnt = c
        with contextlib.ExitStack() as st:
            csem = {e: st.enter_context(nc.semaphore("c_" + e)) for e in ENGS}
            dsem = [st.enter_context(nc.semaphore("d_%d" % i)) for i in range(self.N_DMA_SEMS)]
            block = st.enter_context(nc.Block())

            def run_queue(e, eng):
                waited = {}

                def wait_for(d):
                    if d.dma:
                        key, sem, val = ("d", d.sem), dsem[d.sem], d.semval
                    else:
                        key, sem, val = ("c", d.eng), csem[d.eng], d.cnt
                    if waited.get(key, 0) >= val:
                        return
                    waited[key] = val
                    eng.wait_ge(sem, val)

                def need(d):
                    if d.dma:
                        key, sem, val = ("d", d.sem), dsem[d.sem], d.semval
                    else:
                        key, sem, val = ("c", d.eng), csem[d.eng], d.cnt
                    if waited.get(key, 0) >= val:
                        return None
                    return key, sem, val

                for nd in self.q[e]:
                    fused = None
                    if nd.dma or nd.nofuse or not self.fuse:
                        for d in nd.deps:
                            wait_for(d)
                    elif e == "pe":
                        for d in nd.deps:
                            if id(d) not in nd.pure_ps:
                                wait_for(d)
                        pend = {}
                        for d in nd.deps:
                            if id(d) in nd.pure_ps:
                                r = need(d)
                                if r is not None and (r[0] not in pend or pend[r[0]][2] < r[2]):
                                    pend[r[0]] = r
                        pend = list(pend.values())
                        for (key, sem, val) in pend[:-1]:
                            waited[key] = val
                            eng.wait_ge(sem, val)
                        if pend:
                            fused = pend[-1]
                    else:
                        pend = {}
                        for d in nd.deps:
                            r = need(d)
                            if r is not None and (r[0] not in pend or pend[r[0]][2] < r[2]):
                                pend[r[0]] = r
                        pend = list(pend.values())
                        for (key, sem, val) in pend[:-1]:
                            waited[key] = val
                            eng.wait_ge(sem, val)
                        if pend:
                            fused = pend[-1]
                    ins = nd.fn(eng)
                    if fused is not None:
                        key, sem, val = fused
                        waited[key] = val
                        ins._wait_ge(sem, val)
                    if nd.dma:
                        ins.then_inc(dsem[nd.sem], 16)
                    elif nd.sig:
                        ins.then_inc(csem[e], 1)
                if e == "sp":
                    for d in final_wait:
                        wait_for(d)

            @block.tensor
            def _(eng):
                run_queue("pe", eng)

            @block.scalar
            def _(eng):
                run_queue("act", eng)

            @block.vector
            def _(eng):
                run_queue("dve", eng)

            @block.gpsimd
            def _(eng):
                run_queue("pool", eng)

            @block.sync
            def _(eng):
                run_queue("sp", eng)


def build_program():
    nc = bass.Bass("TRN2", target_bir_lowering=False)

    def din(name, shape):
        return nc.dram_tensor(name, list(shape), F32, kind="ExternalInput").ap()

    x_d = din("x", (SEQ, D))
    xm_d = din("xm", (128, D))
    w_in_d = din("w_in", (D, PW))
    w_out_d = din("w_out", (D, D))
    w_up_d = din("w_up", (D, DFF))
    w_down_d = din("w_down", (DFF, D))
    wgu_d = din("wgu", (16, 256))
    nmw_d = din("nmw", (128, 8))
    nlw_d = din("nlw", (128, 8))
    nfw_d = din("nfw", (128, D))
    bg_d = din("bg", (128, 2))
    gnw_d = din("gnw", (128, 1))
    cw_d = din("cw", (128, 12))
    ident_d = din("ident", (128, 128))
    mask_d = din("mask", (128, 128))
    out_d = nc.dram_tensor("out", [SEQ, D], F32, kind="ExternalOutput").ap()

    ARENA_BYTES = 207 * 1024
    arena = nc.alloc_sbuf_tensor("arena", [128, ARENA_BYTES // 4], F32)
    cur = [0]

    def carve(nbytes, dt, shape=None, at=None, name=None):
        nb = (nbytes + 31) // 32 * 32
        if at is None:
            off = cur[0]
            cur[0] += nb
        else:
            off = at
        assert off % 4 == 0 and off + nb <= ARENA_BYTES, (off, nb)
        v = arena[:, off // 4:(off + nb) // 4]
        if dt is BF16:
            v = v.bitcast(BF16)[:, 0:nbytes // 2]
        else:
            v = v[:, 0:nbytes // 4]
        LAYOUT.setdefault('_list', []).append((off, nbytes))
        return v, off

    def t3(v, a):
        return v.rearrange("p (a b) -> p a b", a=a)

    Hf, _ = carve(NCH * D * 4, F32)
    H = t3(Hf, NCH)
    WINf, off_win = carve(8 * PW * 2, BF16)
    WIN = t3(WINf, 8)
    WOUTf, _ = carve(8 * D * 2, BF16)
    WOUT = t3(WOUTf, 8)
    IDB, _ = carve(128 * 2, BF16)
    MASK, _ = carve(128 * 4, F32)
    SCANM, _ = carve(512 * 4, F32)
    NMW, _ = carve(32, F32)
    NLW, _ = carve(32, F32)
    GNW, _ = carve(32, F32)
    CW, _ = carve(64, F32)
    BG, _ = carve(32, F32)
    NEGB, _ = carve(32, F32)
    WGU, _ = carve(256 * 2, BF16)
    SS, _ = carve(64, F32)
    RSTD, _ = carve(64, F32)
    SS4, _ = carve(32, F32)
    RS4, _ = carve(32, F32)
    CARRY, _ = carve(32, F32)
    Sf, LAYOUT['S'] = carve(2 * 128 * 4, F32)
    S = t3(Sf, 2)
    SBPf, _ = carve(2 * 128 * 2, BF16)
    SBP = t3(SBPf, 2)
    TMPf, _ = carve(2 * 128 * 4, F32)
    TMP = t3(TMPf, 2)
    XSB = [carve(D * 2, BF16)[0] for _ in range(4)]
    work0 = cur[0]
    LAYOUT['work0'] = work0
    HNTf, LAYOUT['HNT'] = carve(8 * 512 * 2, BF16)
    HNT = t3(HNTf, 8)
    E1f, LAYOUT['E1'] = carve(2 * 512 * 4, F32)
    E1 = t3(E1f, 2)
    E2f, LAYOUT['E2'] = carve(2 * 512 * 4, F32)
    E2 = t3(E2f, 2)
    CSf, LAYOUT['CS'] = carve(2 * 512 * 4, F32)
    CS = t3(CSf, 2)
    QZf, LAYOUT['QZ'] = carve(4 * 512 * 2, BF16)
    QZ = t3(QZf, 4)
    KTf, LAYOUT['KT'] = carve(2 * 512 * 2, BF16)
    KT = t3(KTf, 2)
    KTOKf, LAYOUT['KTOK'] = carve(4 * 512 * 2, BF16)
    KTOK = KTOKf.rearrange("p (c h d) -> p c h d", c=4, h=4)
    Vf, LAYOUT['V'] = carve(4 * 512 * 2, BF16)
    V = t3(Vf, 4)
    SGf, LAYOUT['SG'] = carve(4 * 512 * 4, F32)
    SG = t3(SGf, 4)
    ATB = [carve(512 * 2, BF16)[0] for _ in range(2)]
    YB = [carve(512 * 2, BF16)[0] for _ in range(2)]
    YTf, off_yt = carve(8 * 512 * 2, BF16)
    YT = t3(YTf, 8)
    LAYOUT['YT'] = off_yt
    XM, _ = carve(D * 4, F32, at=off_yt)
    HNTMf, _ = carve(8 * 128 * 2, BF16, at=off_yt + D * 4)
    HNTM = t3(HNTMf, 8)
    XSM, _ = carve(D * 2, BF16, at=off_yt + D * 4 + 8 * 128 * 2)
    CC, LAYOUT['CC'] = carve(512 * 4, F32)
    U, LAYOUT['U'] = carve(514 * 4, F32)
    T1, LAYOUT['T1'] = carve(512 * 4, F32)
    GR, LAYOUT['GR'] = carve(512 * 2, BF16)
    endA = cur[0]
    WU, WD = [], []
    o = off_win
    for i in range(2):
        a, _ = carve(8 * 1024 * 2, BF16, at=o)
        WU.append(t3(a, 8))
        o += 8 * 1024 * 2
        a, _ = carve(8 * 1024 * 2, BF16, at=o)
        WD.append(t3(a, 8))
        o += 8 * 1024 * 2
    o = work0
    HN2Tf, _ = carve(8 * SEQ * 2, BF16, at=o)
    HN2T = t3(HN2Tf, 8)
    o += 8 * SEQ * 2
    ACTB = []
    for i in range(2):
        a, _ = carve(8 * 512 * 2, BF16, at=o)
        ACTB.append(t3(a, 8))
        o += 8 * 512 * 2
    RB = []
    for i in range(2):
        a, _ = carve(512 * 4, F32, at=o)
        RB.append(a)
        o += 512 * 4
    NFW, _ = carve(D * 4, F32, at=o)
    o += D * 4
    assert o <= endA, (o, endA)
    LAYOUT['endA'] = endA
    A_KEYS = ["HNT", "E1", "E2", "CS", "QKT", "V", "SG", "AT", "SQ", "ON", "Y",
              "YT", "XM", "CC", "U", "T1", "GR"]

    PS = [nc.alloc_psum_tensor("ps%d" % i, [128, 512], F32) for i in range(8)]
    st = {"ps": 0, "nrot": 5, "psk": 0, "pso": 0}

    def next_ps():
        b = st["ps"] % st["nrot"]
        st["ps"] += 1
        return PS[b], ("ps", b)

    def next_psk():
        return PS[7], ("ps", 7)

    def next_pso():
        b = 5 + st["pso"] % 2
        st["pso"] += 1
        return PS[b], ("ps", b)

    def next_tp():
        ps, key = next_ps()
        return ps[:, :].bitcast(BF16), key

    P = Prog(nc)
    op = P.op

    def dma(eng, out, in_, writes, reads=(), extra=()):
        return op(eng, lambda e: e.dma_start(out=out, in_=in_), reads=reads, writes=writes, dma=True, extra=extra)

    for (dst, src, key) in ((NMW[:, 0:8], nmw_d, "NMW"), (BG[:, 0:2], bg_d, "BG"), (CW[:, 0:12], cw_d, "CW"),
                            (GNW[:, 0:1], gnw_d, "GNW"), (MASK, mask_d, "MASK"), (NLW[:, 0:8], nlw_d, "NLW")):
        dma("sp", dst, src, [key])
    dma("sp", XM, xm_d, ["XM"])
    dma("pool", IDB, ident_d, ["IDB"])
    dma("pool", WGU[0:16, :], wgu_d, ["WGU"])
    def load_x(t, extra=()):
        dma("sp", H[:, 4 * t:4 * t + 4, :], x_d[t * 512:(t + 1) * 512, :].rearrange("(c p) d -> p c d", p=128),
            [("H", 4 * t + c, hf) for c in range(4) for hf in range(2)], extra=extra)

    load_x(0)
    w_in_v = w_in_d.rearrange("(k p) n -> p k n", p=128)
    WIN_PIECES = [(C_GR, C_CC), (C_CC, C_CX), (C_CX, PW), (C_V, C_G), (C_G, C_GR), (C_Q, C_V)]

    def win_key(col):
        for i, (a, b) in enumerate(WIN_PIECES):
            if a <= col < b:
                return ("WIN", i)
        raise ValueError(col)

    for i in (0, 3, 5, 4, 1, 2):
        a, b = WIN_PIECES[i]
        dma("pool", WIN[:, :, a:b], w_in_v[:, :, a:b], [("WIN", i)])
    dma("pool", WOUT, w_out_d.rearrange("(k p) n -> p k n", p=128), ["WOUT"])

    op("pool", lambda e: e.memset(SCANM, 1.0), writes=["SCANM"])
    op("pool", lambda e: e.memset(SCANM.rearrange("p (c t) -> p c t", t=128)[:, :, 0:1], 0.0), writes=["SCANM"])
    op("pool", lambda e: e.memset(Sf, 0.0), writes=["S"])
    op("pool", lambda e: e.memset(SBPf, 0.0), writes=["SB"])
    op("pool", lambda e: e.memset(QZf, 0.0), writes=[("QZ", h) for h in range(4)])
    op("pool", lambda e: e.memset(KTOKf, 0.0), writes=[("KTOK", i) for i in range(4)])
    op("pool", lambda e: e.memset(CARRY[:, 0:8], 0.0), writes=["CARRY"])
    op("dve", lambda e: e.tensor_scalar(NEGB[:, 0:2], BG[:, 0:2], -1.0, None, ALU.mult), reads=["BG"], writes=["NEGB"])

    def rms_rstd(ss_ap, rs_ap, nfeat, keys_r, keys_w):
        op("act", lambda e: e.activation(rs_ap, ss_ap, AF.Ln, bias=EPS, scale=1.0 / nfeat), reads=keys_r, writes=keys_w)
        op("act", lambda e: e.activation(rs_ap, rs_ap, AF.Exp, scale=-0.5), reads=keys_w, writes=keys_w)

    SQJ, _ = carve(D * 2, BF16)

    def norm_chain(src, skeys, slot, part=0, scale_eng="act", xs=None, xkey=None):
        if part in (0, 1):
            op("act", lambda e: e.activation(SQJ, src, AF.Square, accum_out=SS[:, slot:slot + 1]), reads=skeys, writes=[("SS", slot)], nofuse=True)
            rms_rstd(SS[:, slot:slot + 1], RSTD[:, slot:slot + 1], D, [("SS", slot)], [("RSTD", slot)])
        xs = XSB[slot] if xs is None else xs
        xkey = ("XS", slot) if xkey is None else xkey
        if part in (0, 2):
            if scale_eng == "act":
                op("act", lambda e: e.activation(xs, src, AF.Copy, scale=RSTD[:, slot:slot + 1]),
                   reads=skeys + [("RSTD", slot)], writes=[xkey])
            else:
                op("dve", lambda e: e.tensor_scalar(xs, src, RSTD[:, slot:slot + 1], None, ALU.mult),
                   reads=skeys + [("RSTD", slot)], writes=[xkey])

    def transpose_chunk(slot, dst, dkey, wvec, wkey, xs=None, xkey=None):
        tp, tkey = next_tp()
        xb = XSB[slot] if xs is None else xs
        xkey = ("XS", slot) if xkey is None else xkey
        for k in range(8):
            op("pe", lambda e, k=k: e.transpose(tp[:, k * 128:(k + 1) * 128], xb[:, k * 128:(k + 1) * 128], IDB),
               reads=[xkey, "IDB"], writes=[tkey])
        wb = wvec[:, 0:8].unsqueeze(2).broadcast_to([128, 8, 128])
        op("dve", lambda e: e.tensor_tensor(dst, tp[:, :].rearrange("p (k t) -> p k t", k=8), wb, ALU.mult),
           reads=[tkey, wkey], writes=[dkey])

    def h_keys(ch):
        return [("H", ch, 0), ("H", ch, 1)]

    def trans_A(ncht):
        for c in range(ncht):
            transpose_chunk(c, HNT[:, :, c * 128:(c + 1) * 128], ("HNT", c), NMW, "NMW")

    pending = []

    def tile_A(t, meta, next_norm, next_trans):
        ncht = 1 if meta else 4
        TT = ncht * 128
        hnt = HNTM if meta else HNT
        hkey = "HNTM" if meta else "HNT"
        hnt_all = [(hkey, c) for c in range(ncht)]
        if (not meta) and t > 0:
            for c in range(4):
                transpose_chunk(c, HNT[:, :, c * 128:(c + 1) * 128], ("HNT", c), NMW, "NMW")
                yield "slot"

        def inproj_a(col0, m):
            ps, pkey = next_ps()
            wk = win_key(col0)
            for k in range(8):
                op("pe", lambda e, k=k: e.matmul(ps[0:m, 0:TT], WIN[:, k, col0:col0 + m], hnt[:, k, 0:TT],
                                                 start=(k == 0), stop=(k == 7)),
                   reads=hnt_all + [wk], writes=[pkey])
            return ps, pkey

        def inproj_b(col0, c):
            ps, pkey = next_ps()
            wk = win_key(col0)
            for k in range(8):
                op("pe", lambda e, k=k: e.matmul(ps[:, 0:512], hnt[:, k, c * 128:(c + 1) * 128], WIN[:, k, col0:col0 + 512],
                                                 start=(k == 0), stop=(k == 7)),
                   reads=[(hkey, c), wk], writes=[pkey])
            return ps, pkey

        ps, pkey = inproj_a(C_GR, 16)
        op("act", lambda e, ps=ps: e.copy(GR[0:16, 0:TT], ps[0:16, 0:TT]), reads=[pkey], writes=["GR"])

        def gate(c2):
            ps, pkey = next_ps()
            op("pe", lambda e: e.matmul(ps[:, 0:TT], WGU[0:16, c2 * 128:(c2 + 1) * 128], GR[0:16, 0:TT], start=True, stop=True),
               reads=["GR", "WGU"], writes=[pkey])
            op("act", lambda e: e.activation(E1[:, c2, 0:TT], ps[:, 0:TT], AF.Exp, scale=-1.0, bias=NEGB[:, c2:c2 + 1]),
               reads=[pkey, "NEGB"], writes=[("E1", c2)])
            op("act", lambda e: e.activation(E1[:, c2, 0:TT], E1[:, c2, 0:TT], AF.Ln, bias=1.0), reads=[("E1", c2)], writes=[("E1", c2)])
            op("dve", lambda e: e.tensor_tensor_scan(CS[:, c2, 0:TT], SCANM[:, 0:TT], E1[:, c2, 0:TT], 0.0, ALU.mult, ALU.add),
               reads=[("E1", c2), "SCANM"], writes=[("CS", c2)])
            op("act", lambda e: e.activation(E1[:, c2, 0:TT], CS[:, c2, 0:TT], AF.Exp, scale=-1.0 / 16.0),
               reads=[("CS", c2)], writes=[("E1", c2)])
            return op("act", lambda e: e.activation(E2[:, c2, 0:TT], CS[:, c2, 0:TT], AF.Exp, scale=1.0 / 16.0),
                      reads=[("CS", c2)], writes=[("E2", c2)])

        def v_chunk(c):
            ps, pkey = inproj_b(C_V, c)
            op("act", lambda e: e.copy(V[:, c, :], ps[:, 0:512]), reads=[pkey], writes=[("V", c)])

        def g_chunk(c):
            ps, pkey = inproj_b(C_G, c)
            sg = SG[:, c, :]
            k = ("SG", c)
            op("act", lambda e: e.activation(sg, ps[:, 0:512], AF.Exp, scale=-1.0), reads=[pkey], writes=[k])
            op("act", lambda e: e.activation(sg, sg, AF.Ln, bias=1.0), reads=[k], writes=[k])
            op("act", lambda e: e.activation(sg, sg, AF.Exp, scale=-1.0), reads=[k], writes=[k])
            op("dve", lambda e: e.tensor_tensor(sg, ps[:, 0:512], sg, ALU.mult), reads=[pkey, k], writes=[k])

        if not meta:
            yield "slot"
            v_chunk(0)
            yield "slot"
        gate(0)
        g1 = gate(1)
        if meta:
            load_x(1, extra=[g1])
        elif t + 2 < 4:
            load_x(t + 2, extra=[g1])
        if not meta:
            yield "slot"
        for c in range(0 if meta else 1, ncht):
            v_chunk(c)
            if not meta:
                yield "slot"
        if not meta:
            for c in range(ncht):
                g_chunk(c)

        def qk(i):
            ps, pkey = inproj_a((C_Q if i < 2 else C_K) + (i % 2) * 128, 128)
            if i < 2:
                for hh in range(2):
                    r = hh * 64
                    op("dve", lambda e, hh=hh, r=r: e.scalar_tensor_tensor(QZ[r:r + 64, 2 * i + hh, 0:TT], ps[r:r + 64, 0:TT], 0.125,
                                                                            E1[r:r + 64, i, 0:TT], ALU.mult, ALU.mult),
                       reads=[pkey, ("E1", i)], writes=[("QZ", 2 * i + hh)])
            else:
                p = i - 2
                op("dve", lambda e: e.tensor_tensor(KT[:, p, 0:TT], ps[:, 0:TT], E2[:, p, 0:TT], ALU.mult),
                   reads=[pkey, ("E2", p)], writes=[("KT", p)])

        for i in (2, 3, 0, 1):
            if meta and i < 2:
                continue
            qk(i)

        def ktrans(c):
            tp, tkey = next_tp()
            for p in range(2):
                op("pe", lambda e, p=p: e.transpose(tp[:, p * 128:(p + 1) * 128], KT[:, p, c * 128:(c + 1) * 128], IDB),
                   reads=[("KT", p), "IDB"], writes=[tkey])
            dst = KTOKf[:, c * 512:(c + 1) * 512].rearrange("q (p b d) -> q p b d", p=2, b=4)[:, :, 0:4:3, :]
            op("act", lambda e: e.copy(dst, tp[:, 0:256].rearrange("q (p h d) -> q p h d", p=2, h=2)), reads=[tkey], writes=[("KTOK", c)])

        for c in range(ncht):
            ktrans(c)
        chk("pre%d%s" % (t, "m" if meta else ""))

        st_ = {}

        def front(c):
            cs = slice(c * 128, (c + 1) * 128)
            if not meta:
                psA, kA = next_ps()
                for h in range(4):
                    op("pe", lambda e, h=h: e.matmul(psA[:, h * 128:(h + 1) * 128], KT[:, h // 2, cs], QZ[:, h, cs], start=True, stop=True),
                       reads=[("KT", h // 2), ("QZ", h)], writes=[kA])
                at = ATB[c % 2]
                op("dve", lambda e: e.tensor_tensor(at.rearrange("p (h i) -> p h i", h=4), psA[:, :].rearrange("p (h i) -> p h i", h=4),
                                                    MASK.unsqueeze(1).broadcast_to([128, 4, 128]), ALU.mult),
                   reads=[kA, "MASK"], writes=[("AT", c % 2)])

        def mid(c):
            cs = slice(c * 128, (c + 1) * 128)
            psK, kK = next_psk()
            for h in range(4):
                p = h // 2
                op("pe", lambda e, h=h, p=p: e.matmul(psK[:, p * 128:(p + 1) * 128], KTOK[:, c, h, :], V[:, c, h * 128:(h + 1) * 128],
                                                      start=(h % 2 == 0), stop=(h % 2 == 1)),
                   reads=[("KTOK", c), ("V", c)], writes=[kK])
            if not meta:
                psO, kO = next_pso()
                at = ATB[c % 2]
                for h in range(4):
                    op("pe", lambda e, h=h: e.matmul(psO[:, h * 128:(h + 1) * 128], at[:, h * 128:(h + 1) * 128],
                                                     V[:, c, h * 128:(h + 1) * 128], start=True, stop=False),
                       reads=[("AT", c % 2), ("V", c)], writes=[kO])
                    op("pe", lambda e, h=h: e.matmul(psO[:, h * 128:(h + 1) * 128], QZ[:, h, cs], SBP[:, h // 2, :], start=False, stop=True),
                       reads=[("QZ", h), "SB"], writes=[kO])
                st_[("O", c)] = (psO, kO)
            last = c * 128 + 127
            ebl_bc = E1[:, :, last:last + 1].broadcast_to([128, 2, 128])
            op("dve", lambda e: e.tensor_tensor(TMPf, Sf, psK[:, 0:256], ALU.add), reads=["S", kK], writes=["TMP"])
            op("dve", lambda e: e.tensor_tensor(SBP, TMP, ebl_bc, ALU.mult), reads=["TMP", ("E1", 0), ("E1", 1)], writes=["SB"])
            for p in range(2):
                op("act", lambda e, p=p: e.activation(S[:, p, :], TMP[:, p, :], AF.Copy, scale=E1[:, p, last:last + 1]),
                   reads=["TMP", ("E1", p)], writes=["S"])

        def back1(c):
            psO, kO = st_[("O", c)]
            SQ_, G2 = CS[:, 0, :], CS[:, 1, :]
            for h in range(4):
                op("act", lambda e, h=h: e.activation(SQ_[:, h * 128:(h + 1) * 128], psO[:, h * 128:(h + 1) * 128], AF.Square,
                                                      accum_out=SS4[:, h:h + 1]),
                   reads=[kO], writes=[("SS4", h)], nofuse=True)
            rms_rstd(SS4[:, 0:4], RS4[:, 0:4], 128, [("SS4", h) for h in range(4)], ["RS4"])

        def back1b(c):
            psO, kO = st_[("O", c)]
            G2 = CS[:, 1, :]
            op("pool", lambda e: e.tensor_tensor(G2.rearrange("p (h v) -> p h v", h=4), SG[:, c, :].rearrange("p (h v) -> p h v", h=4),
                                                 RS4[:, 0:4].unsqueeze(2).broadcast_to([128, 4, 128]), ALU.mult),
               reads=[("SG", c), "RS4"], writes=[("CS", 1)])
            op("dve", lambda e: e.tensor_tensor(YB[c % 2], psO[:, :], G2, ALU.mult), reads=[kO, ("CS", 1)], writes=[("Y", c % 2)])

        def back2(c):
            cs = slice(c * 128, (c + 1) * 128)
            tpy, ktpy = next_tp()
            y = YB[c % 2]
            for h in range(4):
                op("pe", lambda e, h=h: e.transpose(tpy[:, h * 128:(h + 1) * 128], y[:, h * 128:(h + 1) * 128], IDB),
                   reads=[("Y", c % 2), "IDB"], writes=[ktpy])
            wr = [("YT", h, c) for h in range(4)] + (["XM", ("HNTM", 0), "XSM"] if t == 0 else [])
            op("act", lambda e: e.activation(YT[:, 0:4, cs], tpy[:, 0:512].rearrange("p (h i) -> p h i", h=4), AF.Copy, scale=GNW[:, 0:1]),
               reads=[ktpy, "GNW"], writes=wr)

        def convj(j):
            ps_cc, k_cc = inproj_a(C_CC + j * 128, 128)
            op("act", lambda e: e.copy(CC[:, 0:TT], ps_cc[:, 0:TT]), reads=[k_cc], writes=["CC"])
            ps_cx, k_cx = inproj_a(C_CX + j * 128, 128)
            op("dve", lambda e: e.tensor_copy(U[:, 0:2], CARRY[:, 2 * j:2 * j + 2]), reads=["CARRY"], writes=["U"])
            op("dve", lambda e: e.tensor_tensor(U[:, 2:2 + TT], ps_cx[:, 0:TT], CC[:, 0:TT], ALU.mult), reads=[k_cx, "CC"], writes=["U"])
            op("dve", lambda e: e.tensor_copy(CARRY[:, 2 * j:2 * j + 2], U[:, TT:TT + 2]), reads=["U"], writes=["CARRY"])
            if meta:
                return
            ps_cb, k_cb = inproj_a(C_CB + j * 128, 128)
            op("dve", lambda e: e.tensor_scalar(T1[:, 0:TT], U[:, 2:2 + TT], CW[:, 3 * j + 2:3 * j + 3], None, ALU.mult),
               reads=["U", "CW"], writes=["T1"])
            op("dve", lambda e: e.scalar_tensor_tensor(T1[:, 0:TT], U[:, 1:1 + TT], CW[:, 3 * j + 1:3 * j + 2], T1[:, 0:TT], ALU.mult, ALU.add),
               reads=["U", "CW", "T1"], writes=["T1"])
            op("dve", lambda e: e.scalar_tensor_tensor(T1[:, 0:TT], U[:, 0:TT], CW[:, 3 * j:3 * j + 1], T1[:, 0:TT], ALU.mult, ALU.add),
               reads=["U", "CW", "T1"], writes=["T1"])
            wr = [("YT", 4 + j)] + (["XM", ("HNTM", 0), "XSM"] if t == 0 else [])
            op("dve", lambda e: e.tensor_tensor(YT[:, 4 + j, 0:TT], ps_cb[:, 0:TT], T1[:, 0:TT], ALU.mult), reads=[k_cb, "T1"], writes=wr)

        def outproj(c, hf):
            cs = slice(c * 128, (c + 1) * 128)
            psM, kM = next_ps()
            for f in range(8):
                rd = [("YT", f, c)] if f < 4 else [("YT", f)]
                op("pe", lambda e, f=f: e.matmul(psM[:, 0:512], YT[:, f, cs], WOUT[:, f, hf * 512:(hf + 1) * 512], start=(f == 0), stop=(f == 7)),
                   reads=rd + ["WOUT"], writes=[kM])
            hk = ("H", 4 * t + c, hf)
            hv = H[:, 4 * t + c, hf * 512:(hf + 1) * 512]
            op("dve", lambda e: e.tensor_tensor(hv, hv, psM[:, 0:512], ALU.add), reads=[kM, hk], writes=[hk])

        if meta:
            front(0)
            mid(0)
            yield "split"
            for j in range(4):
                convj(j)
            return
        yield "flush"
        front(0)
        mid(0)
        front(1)
        for c in range(3):
            back1(c)
            convj(c)
            back1b(c)
            if c == 0:
                next_norm(0)
            next_norm(c + 1)
            mid(c + 1)
            if c >= 1:
                back2(c - 1)
            if c + 2 < 4:
                front(c + 2)
            chk("chunk%d_%d" % (t, c))
        back1(3)
        convj(3)
        back1b(3)
        back2(2)
        pending.extend([lambda: outproj(0, 0), lambda: outproj(0, 1), lambda: outproj(1, 0), lambda: outproj(1, 1),
                        lambda: back2(3), lambda: outproj(2, 0), lambda: outproj(2, 1), lambda: outproj(3, 0),
                        lambda: outproj(3, 1)])

    try:
        norm_chain(XM, ["XM"], 4, xs=XSM, xkey="XSM")
        for c in range(4):
            norm_chain(H[:, c, :], h_keys(c), c)
        transpose_chunk(0, HNTM, ("HNTM", 0), NMW, "NMW", xs=XSM, xkey="XSM")
        trans_A(4)
        def drain(n=None):
            while pending and (n is None or n > 0):
                pending.pop(0)()
                if n is not None:
                    n -= 1

        def run(gen, until=None):
            for tag in gen:
                if tag == "slot":
                    drain(1)
                elif tag == "flush":
                    drain()
                if tag == until:
                    return

        gens = []
        for t in range(4):
            if t < 3:
                nn = lambda c, t=t: norm_chain(H[:, 4 * (t + 1) + c, :], h_keys(4 * (t + 1) + c), c)
            else:
                nn = lambda c: norm_chain(H[:, c, :], h_keys(c), c)
            gens.append(tile_A(t, False, nn, None))
        gm = tile_A(0, True, None, None)
        run(gm, until="split")
        run(gens[0], until="flush")
        run(gm)
        run(gens[0])
        for t in range(1, 4):
            run(gens[t])

        st["nrot"] = 8
        win_keys = [("WIN", i) for i in range(len(WIN_PIECES))]
        P.alias(("WU", 0), win_keys)
        P.alias(("WD", 0), win_keys)
        allA = (["XM", ("HNTM", 0), "XSM", "CC", "U", "T1", "GR"] + [("YT", 4 + j) for j in range(4)] +
                [("YT", h, c) for h in range(4) for c in range(4)] + [("HNT", c) for c in range(4)] +
                [(n, c2) for n in ("E1", "E2", "CS") for c2 in range(2)] + [("KT", i) for i in range(2)] +
                [("QZ", i) for i in range(4)] + [("KTOK", i) for i in range(4)] + [("V", c) for c in range(4)] +
                [("SG", c) for c in range(4)] + [("AT", i) for i in range(2)] + [("Y", i) for i in range(2)])
        for t in range(4):
            for c in range(4):
                P.alias(("HN2T", t, c), allA)

        w_up_v = w_up_d.rearrange("(k p) n -> p k n", p=128)

        def load_mlp_weights(p):
            b = p % 2
            dma("pool", WU[b], w_up_v[:, :, p * 1024:(p + 1) * 1024], [("WU", b)])
            dma("pool", WD[b], w_down_d[p * 1024:(p + 1) * 1024, :].rearrange("(j q) n -> q j n", q=128), [("WD", b)])

        load_mlp_weights(0)
        drain(2)
        for c in range(4):
            transpose_chunk(c, HN2T[:, :, c * 128:(c + 1) * 128], ("HN2T", 0, c), NLW, "NLW")
            drain(2)
        drain()
        P.alias(("WU", 1), win_keys + ["WOUT"])
        P.alias(("WD", 1), win_keys + ["WOUT"])
        for i in range(2):
            for j in range(8):
                P.alias(("ACTB", i, j), allA)
            P.alias(("RB", i), allA)
        P.alias("NFW", allA)
        dma("sp", NFW, nfw_d, ["NFW"])

        rb_i = [0]

        def stage1(b, t, j, slot):
            ps, pkey = next_ps()
            for k in range(8):
                op("pe", lambda e, k=k: e.matmul(ps[:, 0:512], WU[b][:, k, j * 128:(j + 1) * 128],
                                                 HN2T[:, k, t * 512:(t + 1) * 512], start=(k == 0), stop=(k == 7)),
                   reads=[("WU", b)] + [("HN2T", t, c_) for c_ in range(4)], writes=[pkey])
            ri = rb_i[0] % 2
            rb_i[0] += 1
            op("act", lambda e: e.activation(RB[ri], ps[:, 0:512], AF.Relu), reads=[pkey], writes=[("RB", ri)])
            op("pool", lambda e: e.tensor_tensor(ACTB[slot][:, j, :], RB[ri], RB[ri], ALU.mult),
               reads=[("RB", ri)], writes=[("ACTB", slot, j)])

        def stage2(b, t, c, hf, slot):
            ps, pkey = next_ps()
            for j in range(8):
                op("pe", lambda e, j=j: e.matmul(ps[:, 0:512], ACTB[slot][:, j, c * 128:(c + 1) * 128],
                                                 WD[b][:, j, hf * 512:(hf + 1) * 512], start=(j == 0), stop=(j == 7)),
                   reads=[("ACTB", slot, j), ("WD", b)], writes=[pkey])
            hk = ("H", 4 * t + c, hf)
            hv = H[:, 4 * t + c, hf * 512:(hf + 1) * 512]
            op("dve", lambda e: e.tensor_tensor(hv, hv, ps[:, 0:512], ALU.add), reads=[pkey, hk], writes=[hk])

        def final_norm(ch):
            hks = h_keys(ch)
            op("act", lambda e: e.activation(SQJ, H[:, ch, :], AF.Square, accum_out=SS[:, ch:ch + 1]), reads=hks, writes=[("SS", ch)], nofuse=True)
            rms_rstd(SS[:, ch:ch + 1], RSTD[:, ch:ch + 1], D, [("SS", ch)], [("RSTD", ch)])
            op("dve", lambda e: e.scalar_tensor_tensor(H[:, ch, :], H[:, ch, :], RSTD[:, ch:ch + 1], NFW, ALU.mult, ALU.mult),
               reads=hks + [("RSTD", ch), "NFW"], writes=hks)

        def store_chunk(ch):
            dma("sp", out_d[ch * 128:(ch + 1) * 128, :], H[:, ch, :], writes=[], reads=h_keys(ch))

        def trans_B(t):
            for c in range(4):
                transpose_chunk(c, HN2T[:, :, t * 512 + c * 128:t * 512 + (c + 1) * 128], ("HN2T", t, c), NLW, "NLW")

        def do_s1(k):
            p, t = divmod(k, 4)
            b, slot = p % 2, k % 2
            for j in range(8):
                stage1(b, t, j, slot)
                if p == 0 and t < 3:
                    ch = 4 * (t + 1) + j // 2
                    norm_chain(H[:, ch, :], h_keys(ch), j // 2, part=1 + j % 2, scale_eng="dve")
            if p == 0 and t < 3:
                trans_B(t + 1)

        def do_s2(k):
            p, t = divmod(k, 4)
            b, slot = p % 2, k % 2
            for c in range(4):
                for hf in range(2):
                    stage2(b, t, c, hf, slot)
                if p == 3:
                    final_norm(4 * t + c)
                    store_chunk(4 * t + c)
            if t == 3 and p + 2 < 4:
                load_mlp_weights(p + 2)

        load_mlp_weights(1)
        for k in range(3):
            do_s1(k)
            do_s2(k)
        do_s1(3)
        for k in range(3, 16):
            if k + 1 < 16:
                do_s1(k + 1)
            do_s2(k)
    except _Stop:
        P.emit(final_wait=list(P.dma_nodes) + [q[-1] for q in P.q.values() if q])
        return nc

    outs = [n for n in P.dma_nodes if n.eng == "sp"][-16:]
    P.emit(final_wait=outs)
    return nc


_NC = None


def kernel(x, meta_tokens, norm_mix_w, w_in, w_gate_up, b_gate, gla_norm_w, conv_w, w_out,
           norm_mlp_w, w_up, w_down, norm_final_w):
    global _NC
    f = lambda a: np.ascontiguousarray(np.asarray(a, dtype=np.float32))
    x = f(x)
    nb = x.shape[0]
    xm = np.zeros((128, D), np.float32)
    xm[112:] = f(meta_tokens)
    shared = {
        "xm": xm,
        "w_in": f(np.asarray(w_in)[0]),
        "w_out": f(np.asarray(w_out)[0]),
        "w_up": f(np.asarray(w_up)[0]),
        "w_down": f(np.asarray(w_down)[0]),
        "wgu": f(np.asarray(w_gate_up)[0]),
        "nmw": f(np.asarray(norm_mix_w)[0].reshape(8, 128).T),
        "nlw": f(np.asarray(norm_mlp_w)[0].reshape(8, 128).T),
        "nfw": f(np.broadcast_to(np.asarray(norm_final_w)[None, :], (128, D))),
        "bg": f(np.asarray(b_gate)[0].reshape(2, 128).T),
        "gnw": f(np.asarray(gla_norm_w)[0].reshape(128, 1)),
        "cw": f(np.asarray(conv_w)[0].reshape(3, 4, 128).transpose(2, 1, 0).reshape(128, 12)),
        "ident": np.eye(128, dtype=np.float32),
        "mask": np.triu(np.ones((128, 128), np.float32)),
    }
    if _NC is None:
        _NC = build_program()
    in_maps = [dict(shared, x=x[b]) for b in range(nb)]
    res = run_bass_kernel_spmd(_NC, in_maps, core_ids=list(range(nb)))
    return np.stack([np.asarray(r["out"], dtype=np.float32) for r in res.results], axis=0)
```

```python
import contextlib

import numpy as np

import concourse.bass as bass
import concourse.mybir as mybir
from concourse.bass_utils import run_bass_kernel_spmd

F32 = mybir.dt.float32
BF16 = mybir.dt.bfloat16
AF = mybir.ActivationFunctionType
ALU = mybir.AluOpType
AX = mybir.AxisListType

D = 1024
SEQ = 2048
NCH = 16
PW = 3088
DFF = 4096
EPS = 1e-6
C_Q, C_K, C_V, C_G, C_GR, C_CB, C_CC, C_CX = 0, 256, 512, 1024, 1536, 1552, 2064, 2576

ENGS = ("pe", "act", "dve", "pool", "sp")
LAYOUT = {}
DEBUG_STOP = None


class _Stop(Exception):
    pass


def chk(name):
    if DEBUG_STOP == name:
        raise _Stop()


class Node:
    __slots__ = ("eng", "fn", "deps", "sig", "cnt", "dma", "sem", "semval", "idx", "psdeps", "nofuse", "pure_ps")

    def __init__(self, eng, fn, dma):
        self.eng = eng
        self.fn = fn
        self.deps = []
        self.sig = False
        self.cnt = 0
        self.dma = dma
        self.sem = None
        self.semval = 0
        self.idx = 0
        self.psdeps = set()
        self.pure_ps = set()
        self.nofuse = False


class Prog:
    N_DMA_SEMS = 36

    def __init__(self, nc):
        self.nc = nc
        self.q = {e: [] for e in ENGS}
        self.last_w = {}
        self.readers = {}
        self.dma_nodes = []
        self.dma_by_q = {}
        self.n = 0
        self.fuse = True

    def alias(self, new, olds):
        r = self.readers.setdefault(new, {})
        for o in olds:
            w = self.last_w.get(o)
            if w is not None:
                r[("w", o)] = w
            for kk, nd in self.readers.get(o, {}).items():
                r[(o, kk)] = nd

    def op(self, eng, fn, reads=(), writes=(), dma=False, extra=(), nofuse=False):
        nd = Node(eng, fn, dma)
        nd.idx = self.n
        nd.nofuse = nofuse
        self.n += 1
        deps = list(extra)
        psd = []
        for k in reads:
            w = self.last_w.get(k)
            if w is not None:
                deps.append(w)
        for k in writes:
            is_ps = isinstance(k, tuple) and k[0] == "ps"
            w = self.last_w.get(k)
            if w is not None:
                deps.append(w)
                if is_ps:
                    psd.append(w)
            rr = list(self.readers.get(k, {}).values())
            deps.extend(rr)
            if is_ps:
                psd.extend(rr)
        nondps = set(id(d) for d in deps) - set(id(d) for d in psd)
        rk = ("dma", nd.idx) if dma else eng
        for k in reads:
            self.readers.setdefault(k, {})[rk] = nd
        for k in writes:
            self.last_w[k] = nd
            self.readers[k] = {}
        if dma:
            lst = self.dma_by_q.setdefault(eng, [])
            base, nslots = (0, 12) if eng == "pool" else (12, self.N_DMA_SEMS - 12)
            i = len(lst)
            if i >= nslots:
                deps.append(lst[i - nslots])
            nd.semval = 16 * (i // nslots + 1)
            nd.sem = base + i % nslots
            lst.append(nd)
            self.dma_nodes.append(nd)
        seen = set()
        for d in deps:
            if d is nd or id(d) in seen:
                continue
            if (not d.dma) and (not dma) and d.eng == "pe" and eng == "pe":
                continue
            seen.add(id(d))
            nd.deps.append(d)
            d.sig = True
        nd.pure_ps = set(id(d) for d in psd) - nondps
        self.q[eng].append(nd)
        return nd

    def emit(self, final_wait=()):
        nc = self.nc
        for d in final_wait:
            d.sig = True
        for e in ENGS:
            c = 0
            for nd in self.q[e]:
                if (not nd.dma) and nd.sig:
                    c += 1
                    nd.cnt = c
        with contextlib.ExitStack() as st:
            csem = {e: st.enter_context(nc.semaphore("c_" + e)) for e in ENGS}
            dsem = [st.enter_context(nc.semaphore("d_%d" % i)) for i in range(self.N_DMA_SEMS)]
            block = st.enter_context(nc.Block())

            def run_queue(e, eng):
                waited = {}

                def wait_for(d):
                    if d.dma:
                        key, sem, val = ("d", d.sem), dsem[d.sem], d.semval
                    else:
                        key, sem, val = ("c", d.eng), csem[d.eng], d.cnt
                    if waited.get(key, 0) >= val:
                        return
                    waited[key] = val
                    eng.wait_ge(sem, val)

                def need(d):
                    if d.dma:
                        key, sem, val = ("d", d.sem), dsem[d.sem], d.semval
                    else:
                        key, sem, val = ("c", d.eng), csem[d.eng], d.cnt
                    if waited.get(key, 0) >= val:
                        return None
                    return key, sem, val

                for nd in self.q[e]:
                    fused = None
                    if nd.dma or nd.nofuse or not self.fuse:
                        for d in nd.deps:
                            wait_for(d)
                    elif e == "pe":
                        for d in nd.deps:
                            if id(d) not in nd.pure_ps:
                                wait_for(d)
                        pend = {}
                        for d in nd.deps:
                            if id(d) in nd.pure_ps:
                                r = need(d)
                                if r is not None and (r[0] not in pend or pend[r[0]][2] < r[2]):
                                    pend[r[0]] = r
                        pend = list(pend.values())
                        for (key, sem, val) in pend[:-1]:
                            waited[key] = val
                            eng.wait_ge(sem, val)
                        if pend:
                            fused = pend[-1]
                    else:
                        pend = {}
                        for d in nd.deps:
                            r = need(d)
                            if r is not None and (r[0] not in pend or pend[r[0]][2] < r[2]):
                                pend[r[0]] = r
                        pend = list(pend.values())
                        for (key, sem, val) in pend[:-1]:
                            waited[key] = val
                            eng.wait_ge(sem, val)
                        if pend:
                            fused = pend[-1]
                    ins = nd.fn(eng)
                    if fused is not None:
                        key, sem, val = fused
                        waited[key] = val
                        ins._wait_ge(sem, val)
                    if nd.dma:
                        ins.then_inc(dsem[nd.sem], 16)
                    elif nd.sig:
                        ins.then_inc(csem[e], 1)
                if e == "sp":
                    for d in final_wait:
                        wait_for(d)

            @block.tensor
            def _(eng):
                run_queue("pe", eng)

            @block.scalar
            def _(eng):
                run_queue("act", eng)

            @block.vector
            def _(eng):
                run_queue("dve", eng)

            @block.gpsimd
            def _(eng):
                run_queue("pool", eng)

            @block.sync
            def _(eng):
                run_queue("sp", eng)


def build_program():
    nc = bass.Bass("TRN2", target_bir_lowering=False)

    def din(name, shape):
        return nc.dram_tensor(name, list(shape), F32, kind="ExternalInput").ap()

    x_d = din("x", (SEQ, D))
    xm_d = din("xm", (128, D))
    w_in_d = din("w_in", (D, PW))
    w_out_d = din("w_out", (D, D))
    w_up_d = din("w_up", (D, DFF))
    w_down_d = din("w_down", (DFF, D))
    wgu_d = din("wgu", (16, 256))
    nmw_d = din("nmw", (128, 8))
    nlw_d = din("nlw", (128, 8))
    nfw_d = din("nfw", (128, D))
    bg_d = din("bg", (128, 2))
    gnw_d = din("gnw", (128, 1))
    cw_d = din("cw", (128, 12))
    ident_d = din("ident", (128, 128))
    mask_d = din("mask", (128, 128))
    out_d = nc.dram_tensor("out", [SEQ, D], F32, kind="ExternalOutput").ap()

    ARENA_BYTES = 207 * 1024
    arena = nc.alloc_sbuf_tensor("arena", [128, ARENA_BYTES // 4], F32)
    cur = [0]

    def carve(nbytes, dt, shape=None, at=None, name=None):
        nb = (nbytes + 31) // 32 * 32
        if at is None:
            off = cur[0]
            cur[0] += nb
        else:
            off = at
        assert off % 4 == 0 and off + nb <= ARENA_BYTES, (off, nb)
        v = arena[:, off // 4:(off + nb) // 4]
        if dt is BF16:
            v = v.bitcast(BF16)[:, 0:nbytes // 2]
        else:
            v = v[:, 0:nbytes // 4]
        LAYOUT.setdefault('_list', []).append((off, nbytes))
        return v, off

    def t3(v, a):
        return v.rearrange("p (a b) -> p a b", a=a)

    Hf, _ = carve(NCH * D * 4, F32)
    H = t3(Hf, NCH)
    WINf, off_win = carve(8 * PW * 2, BF16)
    WIN = t3(WINf, 8)
    WOUTf, _ = carve(8 * D * 2, BF16)
    WOUT = t3(WOUTf, 8)
    IDB, _ = carve(128 * 2, BF16)
    MASK, _ = carve(128 * 4, F32)
    SCANM, _ = carve(512 * 4, F32)
    NMW, _ = carve(32, F32)
    NLW, _ = carve(32, F32)
    GNW, _ = carve(32, F32)
    CW, _ = carve(64, F32)
    BG, _ = carve(32, F32)
    NEGB, _ = carve(32, F32)
    WGU, _ = carve(256 * 2, BF16)
    SS, _ = carve(64, F32)
    RSTD, _ = carve(64, F32)
    SS4, _ = carve(32, F32)
    RS4, _ = carve(32, F32)
    CARRY, _ = carve(32, F32)
    Sf, LAYOUT['S'] = carve(2 * 128 * 4, F32)
    S = t3(Sf, 2)
    SBPf, _ = carve(2 * 128 * 2, BF16)
    SBP = t3(SBPf, 2)
    TMPf, _ = carve(2 * 128 * 4, F32)
    TMP = t3(TMPf, 2)
    XSB = [carve(D * 2, BF16)[0] for _ in range(4)]
    work0 = cur[0]
    LAYOUT['work0'] = work0
    HNTf, LAYOUT['HNT'] = carve(8 * 512 * 2, BF16)
    HNT = t3(HNTf, 8)
    E1f, LAYOUT['E1'] = carve(2 * 512 * 4, F32)
    E1 = t3(E1f, 2)
    E2f, LAYOUT['E2'] = carve(2 * 512 * 4, F32)
    E2 = t3(E2f, 2)
    CSf, LAYOUT['CS'] = carve(2 * 512 * 4, F32)
    CS = t3(CSf, 2)
    QZf, LAYOUT['QZ'] = carve(4 * 512 * 2, BF16)
    QZ = t3(QZf, 4)
    KTf, LAYOUT['KT'] = carve(2 * 512 * 2, BF16)
    KT = t3(KTf, 2)
    KTOKf, LAYOUT['KTOK'] = carve(4 * 512 * 2, BF16)
    KTOK = KTOKf.rearrange("p (c h d) -> p c h d", c=4, h=4)
    Vf, LAYOUT['V'] = carve(4 * 512 * 2, BF16)
    V = t3(Vf, 4)
    SGf, LAYOUT['SG'] = carve(4 * 512 * 4, F32)
    SG = t3(SGf, 4)
    ATB = [carve(512 * 2, BF16)[0] for _ in range(2)]
    YB = [carve(512 * 2, BF16)[0] for _ in range(2)]
    YTf, off_yt = carve(8 * 512 * 2, BF16)
    YT = t3(YTf, 8)
    LAYOUT['YT'] = off_yt
    XM, _ = carve(D * 4, F32, at=off_yt)
    HNTMf, _ = carve(8 * 128 * 2, BF16, at=off_yt + D * 4)
    HNTM = t3(HNTMf, 8)
    XSM, _ = carve(D * 2, BF16, at=off_yt + D * 4 + 8 * 128 * 2)
    CC, LAYOUT['CC'] = carve(512 * 4, F32)
    U, LAYOUT['U'] = carve(514 * 4, F32)
    T1, LAYOUT['T1'] = carve(512 * 4, F32)
    GR, LAYOUT['GR'] = carve(512 * 2, BF16)
    endA = cur[0]
    WU, WD = [], []
    o = off_win
    for i in range(2):
        a, _ = carve(8 * 1024 * 2, BF16, at=o)
        WU.append(t3(a, 8))
        o += 8 * 1024 * 2
        a, _ = carve(8 * 1024 * 2, BF16, at=o)
        WD.append(t3(a, 8))
        o += 8 * 1024 * 2
    o = work0
    HN2Tf, _ = carve(8 * SEQ * 2, BF16, at=o)
    HN2T = t3(HN2Tf, 8)
    o += 8 * SEQ * 2
    ACTB = []
    for i in range(2):
        a, _ = carve(8 * 512 * 2, BF16, at=o)
        ACTB.append(t3(a, 8))
        o += 8 * 512 * 2
    RB = []
    for i in range(2):
        a, _ = carve(512 * 4, F32, at=o)
        RB.append(a)
        o += 512 * 4
    NFW, _ = carve(D * 4, F32, at=o)
    o += D * 4
    assert o <= endA, (o, endA)
    LAYOUT['endA'] = endA
    A_KEYS = ["HNT", "E1", "E2", "CS", "QKT", "V", "SG", "AT", "SQ", "ON", "Y",
              "YT", "XM", "CC", "U", "T1", "GR"]

    PS = [nc.alloc_psum_tensor("ps%d" % i, [128, 512], F32) for i in range(8)]
    st = {"ps": 0, "nrot": 5, "psk": 0, "pso": 0}

    def next_ps():
        b = st["ps"] % st["nrot"]
        st["ps"] += 1
        return PS[b], ("ps", b)

    def next_psk():
        return PS[7], ("ps", 7)

    def next_pso():
        b = 5 + st["pso"] % 2
        st["pso"] += 1
        return PS[b], ("ps", b)

    def next_tp():
        ps, key = next_ps()
        return ps[:, :].bitcast(BF16), key

    P = Prog(nc)
    op = P.op

    def dma(eng, out, in_, writes, reads=(), extra=()):
        return op(eng, lambda e: e.dma_start(out=out, in_=in_), reads=reads, writes=writes, dma=True, extra=extra)

    for (dst, src, key) in ((NMW[:, 0:8], nmw_d, "NMW"), (BG[:, 0:2], bg_d, "BG"), (CW[:, 0:12], cw_d, "CW"),
                            (GNW[:, 0:1], gnw_d, "GNW"), (MASK, mask_d, "MASK"), (NLW[:, 0:8], nlw_d, "NLW")):
        dma("sp", dst, src, [key])
    dma("sp", XM, xm_d, ["XM"])
    dma("pool", IDB, ident_d, ["IDB"])
    dma("pool", WGU[0:16, :], wgu_d, ["WGU"])
    def load_x(t, extra=()):
        dma("sp", H[:, 4 * t:4 * t + 4, :], x_d[t * 512:(t + 1) * 512, :].rearrange("(c p) d -> p c d", p=128),
            [("H", 4 * t + c, hf) for c in range(4) for hf in range(2)], extra=extra)

    load_x(0)
    w_in_v = w_in_d.rearrange("(k p) n -> p k n", p=128)
    WIN_PIECES = [(C_GR, C_CC), (C_CC, C_CX), (C_CX, PW), (C_V, C_G), (C_G, C_GR), (C_Q, C_V)]

    def win_key(col):
        for i, (a, b) in enumerate(WIN_PIECES):
            if a <= col < b:
                return ("WIN", i)
        raise ValueError(col)

    for i in (0, 3, 5, 4, 1, 2):
        a, b = WIN_PIECES[i]
        dma("pool", WIN[:, :, a:b], w_in_v[:, :, a:b], [("WIN", i)])
    dma("pool", WOUT, w_out_d.rearrange("(k p) n -> p k n", p=128), ["WOUT"])

    op("pool", lambda e: e.memset(SCANM, 1.0), writes=["SCANM"])
    op("pool", lambda e: e.memset(SCANM.rearrange("p (c t) -> p c t", t=128)[:, :, 0:1], 0.0), writes=["SCANM"])
    op("pool", lambda e: e.memset(Sf, 0.0), writes=["S"])
    op("pool", lambda e: e.memset(SBPf, 0.0), writes=["SB"])
    op("pool", lambda e: e.memset(QZf, 0.0), writes=[("QZ", h) for h in range(4)])
    op("pool", lambda e: e.memset(KTOKf, 0.0), writes=[("KTOK", i) for i in range(4)])
    op("pool", lambda e: e.memset(CARRY[:, 0:8], 0.0), writes=["CARRY"])
    op("dve", lambda e: e.tensor_scalar(NEGB[:, 0:2], BG[:, 0:2], -1.0, None, ALU.mult), reads=["BG"], writes=["NEGB"])

    def rms_rstd(ss_ap, rs_ap, nfeat, keys_r, keys_w):
        op("act", lambda e: e.activation(rs_ap, ss_ap, AF.Ln, bias=EPS, scale=1.0 / nfeat), reads=keys_r, writes=keys_w)
        op("act", lambda e: e.activation(rs_ap, rs_ap, AF.Exp, scale=-0.5), reads=keys_w, writes=keys_w)

    SQJ, _ = carve(D * 2, BF16)

    def norm_chain(src, skeys, slot, part=0, scale_eng="act", xs=None, xkey=None):
        if part in (0, 1):
            op("act", lambda e: e.activation(SQJ, src, AF.Square, accum_out=SS[:, slot:slot + 1]), reads=skeys, writes=[("SS", slot)], nofuse=True)
            rms_rstd(SS[:, slot:slot + 1], RSTD[:, slot:slot + 1], D, [("SS", slot)], [("RSTD", slot)])
        xs = XSB[slot] if xs is None else xs
        xkey = ("XS", slot) if xkey is None else xkey
        if part in (0, 2):
            if scale_eng == "act":
                op("act", lambda e: e.activation(xs, src, AF.Copy, scale=RSTD[:, slot:slot + 1]),
                   reads=skeys + [("RSTD", slot)], writes=[xkey])
            else:
                op("dve", lambda e: e.tensor_scalar(xs, src, RSTD[:, slot:slot + 1], None, ALU.mult),
                   reads=skeys + [("RSTD", slot)], writes=[xkey])

    def transpose_chunk(slot, dst, dkey, wvec, wkey, xs=None, xkey=None):
        tp, tkey = next_tp()
        xb = XSB[slot] if xs is None else xs
        xkey = ("XS", slot) if xkey is None else xkey
        for k in range(8):
            op("pe", lambda e, k=k: e.transpose(tp[:, k * 128:(k + 1) * 128], xb[:, k * 128:(k + 1) * 128], IDB),
               reads=[xkey, "IDB"], writes=[tkey])
        wb = wvec[:, 0:8].unsqueeze(2).broadcast_to([128, 8, 128])
        op("dve", lambda e: e.tensor_tensor(dst, tp[:, :].rearrange("p (k t) -> p k t", k=8), wb, ALU.mult),
           reads=[tkey, wkey], writes=[dkey])

    def h_keys(ch):
        return [("H", ch, 0), ("H", ch, 1)]

    def trans_A(ncht):
        for c in range(ncht):
            transpose_chunk(c, HNT[:, :, c * 128:(c + 1) * 128], ("HNT", c), NMW, "NMW")

    pending = []

    def tile_A(t, meta, next_norm, next_trans):
        ncht = 1 if meta else 4
        TT = ncht * 128
        hnt = HNTM if meta else HNT
        hkey = "HNTM" if meta else "HNT"
        hnt_all = [(hkey, c) for c in range(ncht)]
        if (not meta) and t > 0:
            for c in range(4):
                transpose_chunk(c, HNT[:, :, c * 128:(c + 1) * 128], ("HNT", c), NMW, "NMW")
                yield "slot"

        def inproj_a(col0, m):
            ps, pkey = next_ps()
            wk = win_key(col0)
            for k in range(8):
                op("pe", lambda e, k=k: e.matmul(ps[0:m, 0:TT], WIN[:, k, col0:col0 + m], hnt[:, k, 0:TT],
                                                 start=(k == 0), stop=(k == 7)),
                   reads=hnt_all + [wk], writes=[pkey])
            return ps, pkey

        def inproj_b(col0, c):
            ps, pkey = next_ps()
            wk = win_key(col0)
            for k in range(8):
                op("pe", lambda e, k=k: e.matmul(ps[:, 0:512], hnt[:, k, c * 128:(c + 1) * 128], WIN[:, k, col0:col0 + 512],
                                                 start=(k == 0), stop=(k == 7)),
                   reads=[(hkey, c), wk], writes=[pkey])
            return ps, pkey

        ps, pkey = inproj_a(C_GR, 16)
        op("act", lambda e, ps=ps: e.copy(GR[0:16, 0:TT], ps[0:16, 0:TT]), reads=[pkey], writes=["GR"])

        def gate(c2):
            ps, pkey = next_ps()
            op("pe", lambda e: e.matmul(ps[:, 0:TT], WGU[0:16, c2 * 128:(c2 + 1) * 128], GR[0:16, 0:TT], start=True, stop=True),
               reads=["GR", "WGU"], writes=[pkey])
            op("act", lambda e: e.activation(E1[:, c2, 0:TT], ps[:, 0:TT], AF.Exp, scale=-1.0, bias=NEGB[:, c2:c2 + 1]),
               reads=[pkey, "NEGB"], writes=[("E1", c2)])
            op("act", lambda e: e.activation(E1[:, c2, 0:TT], E1[:, c2, 0:TT], AF.Ln, bias=1.0), reads=[("E1", c2)], writes=[("E1", c2)])
            op("dve", lambda e: e.tensor_tensor_scan(CS[:, c2, 0:TT], SCANM[:, 0:TT], E1[:, c2, 0:TT], 0.0, ALU.mult, ALU.add),
               reads=[("E1", c2), "SCANM"], writes=[("CS", c2)])
            op("act", lambda e: e.activation(E1[:, c2, 0:TT], CS[:, c2, 0:TT], AF.Exp, scale=-1.0 / 16.0),
               reads=[("CS", c2)], writes=[("E1", c2)])
            return op("act", lambda e: e.activation(E2[:, c2, 0:TT], CS[:, c2, 0:TT], AF.Exp, scale=1.0 / 16.0),
                      reads=[("CS", c2)], writes=[("E2", c2)])

        def v_chunk(c):
            ps, pkey = inproj_b(C_V, c)
            op("act", lambda e: e.copy(V[:, c, :], ps[:, 0:512]), reads=[pkey], writes=[("V", c)])

        def g_chunk(c):
            ps, pkey = inproj_b(C_G, c)
            sg = SG[:, c, :]
            k = ("SG", c)
            op("act", lambda e: e.activation(sg, ps[:, 0:512], AF.Exp, scale=-1.0), reads=[pkey], writes=[k])
            op("act", lambda e: e.activation(sg, sg, AF.Ln, bias=1.0), reads=[k], writes=[k])
            op("act", lambda e: e.activation(sg, sg, AF.Exp, scale=-1.0), reads=[k], writes=[k])
            op("dve", lambda e: e.tensor_tensor(sg, ps[:, 0:512], sg, ALU.mult), reads=[pkey, k], writes=[k])

        if not meta:
            yield "slot"
            v_chunk(0)
            yield "slot"
        gate(0)
        g1 = gate(1)
        if meta:
            load_x(1, extra=[g1])
        elif t + 2 < 4:
            load_x(t + 2, extra=[g1])
        if not meta:
            yield "slot"
        for c in range(0 if meta else 1, ncht):
            v_chunk(c)
            if not meta:
                yield "slot"
        if not meta:
            for c in range(ncht):
                g_chunk(c)

        def qk(i):
            ps, pkey = inproj_a((C_Q if i < 2 else C_K) + (i % 2) * 128, 128)
            if i < 2:
                for hh in range(2):
                    r = hh * 64
                    op("dve", lambda e, hh=hh, r=r: e.scalar_tensor_tensor(QZ[r:r + 64, 2 * i + hh, 0:TT], ps[r:r + 64, 0:TT], 0.125,
                                                                            E1[r:r + 64, i, 0:TT], ALU.mult, ALU.mult),
                       reads=[pkey, ("E1", i)], writes=[("QZ", 2 * i + hh)])
            else:
                p = i - 2
                op("dve", lambda e: e.tensor_tensor(KT[:, p, 0:TT], ps[:, 0:TT], E2[:, p, 0:TT], ALU.mult),
                   reads=[pkey, ("E2", p)], writes=[("KT", p)])

        for i in (2, 3, 0, 1):
            if meta and i < 2:
                continue
            qk(i)

        def ktrans(c):
            tp, tkey = next_tp()
            for p in range(2):
                op("pe", lambda e, p=p: e.transpose(tp[:, p * 128:(p + 1) * 128], KT[:, p, c * 128:(c + 1) * 128], IDB),
                   reads=[("KT", p), "IDB"], writes=[tkey])
            dst = KTOKf[:, c * 512:(c + 1) * 512].rearrange("q (p b d) -> q p b d", p=2, b=4)[:, :, 0:4:3, :]
            op("act", lambda e: e.copy(dst, tp[:, 0:256].rearrange("q (p h d) -> q p h d", p=2, h=2)), reads=[tkey], writes=[("KTOK", c)])

        for c in range(ncht):
            ktrans(c)
        chk("pre%d%s" % (t, "m" if meta else ""))

        st_ = {}

        def front(c):
            cs = slice(c * 128, (c + 1) * 128)
            if not meta:
                psA, kA = next_ps()
                for h in range(4):
                    op("pe", lambda e, h=h: e.matmul(psA[:, h * 128:(h + 1) * 128], KT[:, h // 2, cs], QZ[:, h, cs], start=True, stop=True),
                       reads=[("KT", h // 2), ("QZ", h)], writes=[kA])
                at = ATB[c % 2]
                op("dve", lambda e: e.tensor_tensor(at.rearrange("p (h i) -> p h i", h=4), psA[:, :].rearrange("p (h i) -> p h i", h=4),
                                                    MASK.unsqueeze(1).broadcast_to([128, 4, 128]), ALU.mult),
                   reads=[kA, "MASK"], writes=[("AT", c % 2)])

        def mid(c):
            cs = slice(c * 128, (c + 1) * 128)
            psK, kK = next_psk()
            for h in range(4):
                p = h // 2
                op("pe", lambda e, h=h, p=p: e.matmul(psK[:, p * 128:(p + 1) * 128], KTOK[:, c, h, :], V[:, c, h * 128:(h + 1) * 128],
                                                      start=(h % 2 == 0), stop=(h % 2 == 1)),
                   reads=[("KTOK", c), ("V", c)], writes=[kK])
            if not meta:
                psO, kO = next_pso()
                at = ATB[c % 2]
                for h in range(4):
                    op("pe", lambda e, h=h: e.matmul(psO[:, h * 128:(h + 1) * 128], at[:, h * 128:(h + 1) * 128],
                                                     V[:, c, h * 128:(h + 1) * 128], start=True, stop=False),
                       reads=[("AT", c % 2), ("V", c)], writes=[kO])
                    op("pe", lambda e, h=h: e.matmul(psO[:, h * 128:(h + 1) * 128], QZ[:, h, cs], SBP[:, h // 2, :], start=False, stop=True),
                       reads=[("QZ", h), "SB"], writes=[kO])
                st_[("O", c)] = (psO, kO)
            last = c * 128 + 127
            ebl_bc = E1[:, :, last:last + 1].broadcast_to([128, 2, 128])
            op("dve", lambda e: e.tensor_tensor(TMPf, Sf, psK[:, 0:256], ALU.add), reads=["S", kK], writes=["TMP"])
            op("dve", lambda e: e.tensor_tensor(SBP, TMP, ebl_bc, ALU.mult), reads=["TMP", ("E1", 0), ("E1", 1)], writes=["SB"])
            op("dve", lambda e: e.tensor_tensor(S, TMP, ebl_bc, ALU.mult), reads=["TMP", ("E1", 0), ("E1", 1)], writes=["S"])

        def back1(c):
            psO, kO = st_[("O", c)]
            SQ_, G2 = CS[:, 0, :], CS[:, 1, :]
            for h in range(4):
                op("act", lambda e, h=h: e.activation(SQ_[:, h * 128:(h + 1) * 128], psO[:, h * 128:(h + 1) * 128], AF.Square,
                                                      accum_out=SS4[:, h:h + 1]),
                   reads=[kO], writes=[("SS4", h)], nofuse=True)
            rms_rstd(SS4[:, 0:4], RS4[:, 0:4], 128, [("SS4", h) for h in range(4)], ["RS4"])

        def back1b(c):
            psO, kO = st_[("O", c)]
            G2 = CS[:, 1, :]
            op("pool", lambda e: e.tensor_tensor(G2.rearrange("p (h v) -> p h v", h=4), SG[:, c, :].rearrange("p (h v) -> p h v", h=4),
                                                 RS4[:, 0:4].unsqueeze(2).broadcast_to([128, 4, 128]), ALU.mult),
               reads=[("SG", c), "RS4"], writes=[("CS", 1)])
            op("dve", lambda e: e.tensor_tensor(YB[c % 2], psO[:, :], G2, ALU.mult), reads=[kO, ("CS", 1)], writes=[("Y", c % 2)])

        def back2(c):
            cs = slice(c * 128, (c + 1) * 128)
            tpy, ktpy = next_tp()
            y = YB[c % 2]
            for h in range(4):
                op("pe", lambda e, h=h: e.transpose(tpy[:, h * 128:(h + 1) * 128], y[:, h * 128:(h + 1) * 128], IDB),
                   reads=[("Y", c % 2), "IDB"], writes=[ktpy])
            wr = [("YT", h, c) for h in range(4)] + (["XM", ("HNTM", 0), "XSM"] if t == 0 else [])
            op("act", lambda e: e.activation(YT[:, 0:4, cs], tpy[:, 0:512].rearrange("p (h i) -> p h i", h=4), AF.Copy, scale=GNW[:, 0:1]),
               reads=[ktpy, "GNW"], writes=wr)

        def convj(j):
            ps_cc, k_cc = inproj_a(C_CC + j * 128, 128)
            op("act", lambda e: e.copy(CC[:, 0:TT], ps_cc[:, 0:TT]), reads=[k_cc], writes=["CC"])
            ps_cx, k_cx = inproj_a(C_CX + j * 128, 128)
            op("dve", lambda e: e.tensor_copy(U[:, 0:2], CARRY[:, 2 * j:2 * j + 2]), reads=["CARRY"], writes=["U"])
            op("dve", lambda e: e.tensor_tensor(U[:, 2:2 + TT], ps_cx[:, 0:TT], CC[:, 0:TT], ALU.mult), reads=[k_cx, "CC"], writes=["U"])
            op("dve", lambda e: e.tensor_copy(CARRY[:, 2 * j:2 * j + 2], U[:, TT:TT + 2]), reads=["U"], writes=["CARRY"])
            if meta:
                return
            ps_cb, k_cb = inproj_a(C_CB + j * 128, 128)
            op("dve", lambda e: e.tensor_scalar(T1[:, 0:TT], U[:, 2:2 + TT], CW[:, 3 * j + 2:3 * j + 3], None, ALU.mult),
               reads=["U", "CW"], writes=["T1"])
            op("dve", lambda e: e.scalar_tensor_tensor(T1[:, 0:TT], U[:, 1:1 + TT], CW[:, 3 * j + 1:3 * j + 2], T1[:, 0:TT], ALU.mult, ALU.add),
               reads=["U", "CW", "T1"], writes=["T1"])
            op("dve", lambda e: e.scalar_tensor_tensor(T1[:, 0:TT], U[:, 0:TT], CW[:, 3 * j:3 * j + 1], T1[:, 0:TT], ALU.mult, ALU.add),
               reads=["U", "CW", "T1"], writes=["T1"])
            wr = [("YT", 4 + j)] + (["XM", ("HNTM", 0), "XSM"] if t == 0 else [])
            op("dve", lambda e: e.tensor_tensor(YT[:, 4 + j, 0:TT], ps_cb[:, 0:TT], T1[:, 0:TT], ALU.mult), reads=[k_cb, "T1"], writes=wr)

        def outproj(c, hf):
            cs = slice(c * 128, (c + 1) * 128)
            psM, kM = next_ps()
            for f in range(8):
                rd = [("YT", f, c)] if f < 4 else [("YT", f)]
                op("pe", lambda e, f=f: e.matmul(psM[:, 0:512], YT[:, f, cs], WOUT[:, f, hf * 512:(hf + 1) * 512], start=(f == 0), stop=(f == 7)),
                   reads=rd + ["WOUT"], writes=[kM])
            hk = ("H", 4 * t + c, hf)
            hv = H[:, 4 * t + c, hf * 512:(hf + 1) * 512]
            op("dve", lambda e: e.tensor_tensor(hv, hv, psM[:, 0:512], ALU.add), reads=[kM, hk], writes=[hk])

        if meta:
            front(0)
            mid(0)
            yield "split"
            for j in range(4):
                convj(j)
            return
        yield "flush"
        front(0)
        mid(0)
        front(1)
        for c in range(3):
            back1(c)
            convj(c)
            back1b(c)
            if c == 0:
                next_norm(0)
            next_norm(c + 1)
            mid(c + 1)
            if c >= 1:
                back2(c - 1)
            if c + 2 < 4:
                front(c + 2)
            chk("chunk%d_%d" % (t, c))
        back1(3)
        convj(3)
        back1b(3)
        back2(2)
        pending.extend([lambda: outproj(0, 0), lambda: outproj(0, 1), lambda: outproj(1, 0), lambda: outproj(1, 1),
                        lambda: back2(3), lambda: outproj(2, 0), lambda: outproj(2, 1), lambda: outproj(3, 0),
                        lambda: outproj(3, 1)])

    try:
        norm_chain(XM, ["XM"], 4, xs=XSM, xkey="XSM")
        for c in range(4):
            norm_chain(H[:, c, :], h_keys(c), c, scale_eng="dve")
        transpose_chunk(0, HNTM, ("HNTM", 0), NMW, "NMW", xs=XSM, xkey="XSM")
        trans_A(4)
        def drain(n=None):
            while pending and (n is None or n > 0):
                pending.pop(0)()
                if n is not None:
                    n -= 1

        def run(gen, until=None):
            for tag in gen:
                if tag == "slot":
                    drain(1)
                elif tag == "flush":
                    drain()
                if tag == until:
                    return

        gens = []
        for t in range(4):
            if t < 3:
                nn = lambda c, t=t: norm_chain(H[:, 4 * (t + 1) + c, :], h_keys(4 * (t + 1) + c), c)
            else:
                nn = lambda c: norm_chain(H[:, c, :], h_keys(c), c)
            gens.append(tile_A(t, False, nn, None))
        gm = tile_A(0, True, None, None)
        run(gm, until="split")
        run(gens[0], until="flush")
        run(gm)
        run(gens[0])
        for t in range(1, 4):
            run(gens[t])

        st["nrot"] = 8
        win_keys = [("WIN", i) for i in range(len(WIN_PIECES))]
        P.alias(("WU", 0), win_keys)
        P.alias(("WD", 0), win_keys)
        allA = (["XM", ("HNTM", 0), "XSM", "CC", "U", "T1", "GR"] + [("YT", 4 + j) for j in range(4)] +
                [("YT", h, c) for h in range(4) for c in range(4)] + [("HNT", c) for c in range(4)] +
                [(n, c2) for n in ("E1", "E2", "CS") for c2 in range(2)] + [("KT", i) for i in range(2)] +
                [("QZ", i) for i in range(4)] + [("KTOK", i) for i in range(4)] + [("V", c) for c in range(4)] +
                [("SG", c) for c in range(4)] + [("AT", i) for i in range(2)] + [("Y", i) for i in range(2)])
        for t in range(4):
            for c in range(4):
                P.alias(("HN2T", t, c), allA)

        w_up_v = w_up_d.rearrange("(k p) n -> p k n", p=128)

        def load_mlp_weights(p):
            b = p % 2
            dma("pool", WU[b], w_up_v[:, :, p * 1024:(p + 1) * 1024], [("WU", b)])
            dma("pool", WD[b], w_down_d[p * 1024:(p + 1) * 1024, :].rearrange("(j q) n -> q j n", q=128), [("WD", b)])

        load_mlp_weights(0)
        drain(2)
        for c in range(4):
            transpose_chunk(c, HN2T[:, :, c * 128:(c + 1) * 128], ("HN2T", 0, c), NLW, "NLW")
            drain(2)
        drain()
        P.alias(("WU", 1), win_keys + ["WOUT"])
        P.alias(("WD", 1), win_keys + ["WOUT"])
        for i in range(2):
            for j in range(8):
                P.alias(("ACTB", i, j), allA)
            P.alias(("RB", i), allA)
        P.alias("NFW", allA)
        dma("sp", NFW, nfw_d, ["NFW"])

        rb_i = [0]

        def stage1(b, t, j, slot):
            ps, pkey = next_ps()
            for k in range(8):
                op("pe", lambda e, k=k: e.matmul(ps[:, 0:512], WU[b][:, k, j * 128:(j + 1) * 128],
                                                 HN2T[:, k, t * 512:(t + 1) * 512], start=(k == 0), stop=(k == 7)),
                   reads=[("WU", b)] + [("HN2T", t, c_) for c_ in range(4)], writes=[pkey])
            ri = rb_i[0] % 2
            rb_i[0] += 1
            op("act", lambda e: e.activation(RB[ri], ps[:, 0:512], AF.Relu), reads=[pkey], writes=[("RB", ri)])
            op("pool", lambda e: e.tensor_tensor(ACTB[slot][:, j, :], RB[ri], RB[ri], ALU.mult),
               reads=[("RB", ri)], writes=[("ACTB", slot, j)])

        def stage2(b, t, c, hf, slot):
            ps, pkey = next_ps()
            for j in range(8):
                op("pe", lambda e, j=j: e.matmul(ps[:, 0:512], ACTB[slot][:, j, c * 128:(c + 1) * 128],
                                                 WD[b][:, j, hf * 512:(hf + 1) * 512], start=(j == 0), stop=(j == 7)),
                   reads=[("ACTB", slot, j), ("WD", b)], writes=[pkey])
            hk = ("H", 4 * t + c, hf)
            hv = H[:, 4 * t + c, hf * 512:(hf + 1) * 512]
            op("dve", lambda e: e.tensor_tensor(hv, hv, ps[:, 0:512], ALU.add), reads=[pkey, hk], writes=[hk])

        def final_norm(ch):
            hks = h_keys(ch)
            op("act", lambda e: e.activation(SQJ, H[:, ch, :], AF.Square, accum_out=SS[:, ch:ch + 1]), reads=hks, writes=[("SS", ch)], nofuse=True)
            rms_rstd(SS[:, ch:ch + 1], RSTD[:, ch:ch + 1], D, [("SS", ch)], [("RSTD", ch)])
            op("dve", lambda e: e.scalar_tensor_tensor(H[:, ch, :], H[:, ch, :], RSTD[:, ch:ch + 1], NFW, ALU.mult, ALU.mult),
               reads=hks + [("RSTD", ch), "NFW"], writes=hks)

        def store_chunk(ch):
            dma("sp", out_d[ch * 128:(ch + 1) * 128, :], H[:, ch, :], writes=[], reads=h_keys(ch))

        def trans_B(t):
            for c in range(4):
                transpose_chunk(c, HN2T[:, :, t * 512 + c * 128:t * 512 + (c + 1) * 128], ("HN2T", t, c), NLW, "NLW")

        def do_s1(k):
            p, t = divmod(k, 4)
            b, slot = p % 2, k % 2
            for j in range(8):
                stage1(b, t, j, slot)
                if p == 0 and t < 3:
                    ch = 4 * (t + 1) + j // 2
                    norm_chain(H[:, ch, :], h_keys(ch), j // 2, part=1 + j % 2, scale_eng="dve")
            if p == 0 and t < 3:
                trans_B(t + 1)

        def do_s2(k):
            p, t = divmod(k, 4)
            b, slot = p % 2, k % 2
            for c in range(4):
                for hf in range(2):
                    stage2(b, t, c, hf, slot)
                if p == 3:
                    final_norm(4 * t + c)
                    store_chunk(4 * t + c)
            if t == 3 and p + 2 < 4:
                load_mlp_weights(p + 2)

        load_mlp_weights(1)
        for k in range(3):
            do_s1(k)
            do_s2(k)
        do_s1(3)
        for k in range(3, 16):
            if k + 1 < 16:
                do_s1(k + 1)
            do_s2(k)
    except _Stop:
        P.emit(final_wait=list(P.dma_nodes) + [q[-1] for q in P.q.values() if q])
        return nc

    outs = [n for n in P.dma_nodes if n.eng == "sp"][-16:]
    P.emit(final_wait=outs)
    return nc


_NC = None


def kernel(x, meta_tokens, norm_mix_w, w_in, w_gate_up, b_gate, gla_norm_w, conv_w, w_out,
           norm_mlp_w, w_up, w_down, norm_final_w):
    global _NC
    f = lambda a: np.ascontiguousarray(np.asarray(a, dtype=np.float32))
    x = f(x)
    nb = x.shape[0]
    xm = np.zeros((128, D), np.float32)
    xm[112:] = f(meta_tokens)
    shared = {
        "xm": xm,
        "w_in": f(np.asarray(w_in)[0]),
        "w_out": f(np.asarray(w_out)[0]),
        "w_up": f(np.asarray(w_up)[0]),
        "w_down": f(np.asarray(w_down)[0]),
        "wgu": f(np.asarray(w_gate_up)[0]),
        "nmw": f(np.asarray(norm_mix_w)[0].reshape(8, 128).T),
        "nlw": f(np.asarray(norm_mlp_w)[0].reshape(8, 128).T),
        "nfw": f(np.broadcast_to(np.asarray(norm_final_w)[None, :], (128, D))),
        "bg": f(np.asarray(b_gate)[0].reshape(2, 128).T),
        "gnw": f(np.asarray(gla_norm_w)[0].reshape(128, 1)),
        "cw": f(np.asarray(conv_w)[0].reshape(3, 4, 128).transpose(2, 1, 0).reshape(128, 12)),
        "ident": np.eye(128, dtype=np.float32),
        "mask": np.triu(np.ones((128, 128), np.float32)),
    }
    if _NC is None:
        _NC = build_program()
    in_maps = [dict(shared, x=x[b]) for b in range(nb)]
    res = run_bass_kernel_spmd(_NC, in_maps, core_ids=list(range(nb)))
    return np.stack([np.asarray(r["out"], dtype=np.float32) for r in res.results], axis=0)
```
